# Optimizing a Trainium2 kernel written in Bass

```python
import jax, jax.numpy as jnp
from jax import lax
import numpy as np

D_MODEL = 1024
BATCH = 8
SEQ = 4096
DEPTH = 1

D_MIX = D_MODEL
W_A = D_MIX // 2
HA_HEAD_DIM = 128
HA_HEADS = W_A // HA_HEAD_DIM
W_B = D_MIX - W_A
HB_HEAD_DIM = 64
HB_HEADS = W_B // HB_HEAD_DIM
HGRN_CHUNK = 64
DECAY_LORA = max(32, int(round(1.8 * W_B ** 0.5 / 32)) * 32)
AAA_LORA = max(32, int(round(1.8 * W_B ** 0.5 / 32)) * 32)
GATE_LORA = max(32, int(round(0.6 * W_B ** 0.8 / 32)) * 32)
N_HGRN_COLS = 4 * W_A
N_RWKV_COLS = 3 * W_B + DECAY_LORA + AAA_LORA + GATE_LORA
D_IN_PROJ = N_HGRN_COLS + N_RWKV_COLS
D_FF = ((8 * D_MODEL // 3 + 255) // 256) * 256
NORM_EPS = 1e-6
RWKV_GN_EPS = 64e-5
L2_EPS = 1e-12

kernel_name = 'hybrid_hgrn2_rwkv7_macaron'


def rms_norm(x, g):
    xf = x.astype(jnp.float32)
    y = xf * lax.rsqrt(jnp.mean(xf * xf, axis=-1, keepdims=True) + NORM_EPS)
    return (y * g).astype(x.dtype)


def swiglu(h, w_gate, w_up, w_down):
    return (jax.nn.silu(h @ w_gate) * (h @ w_up)) @ w_down


def token_shift(t):
    return jnp.pad(t, ((0, 0), (1, 0), (0, 0)))[:, :-1]


def hgrn2_chunkwise(q, k, v, log_f):
    B, T, H, K = q.shape
    V = v.shape[-1]
    C = HGRN_CHUNK
    n = T // C

    def blocks(t):
        return t.reshape(B, n, C, H, t.shape[-1]).transpose(0, 3, 1, 2, 4)

    q, k, v, log_f = blocks(q), blocks(k), blocks(v), blocks(log_f)
    b = jnp.cumsum(log_f, axis=3)
    b_ref = b[:, :, :, C // 2:C // 2 + 1]
    b_last = b[:, :, :, C - 1:]
    scores = jnp.einsum('bhntk,bhnsk->bhnts', q * jnp.exp(b - b_ref), k * jnp.exp(b_ref - b))
    causal = jnp.tril(jnp.ones((C, C), dtype=bool))
    scores = jnp.where(causal, scores, 0.0)
    o = jnp.einsum('bhnts,bhnsv->bhntv', scores, v)
    u = jnp.einsum('bhnsk,bhnsv->bhnkv', k * jnp.exp(b_last - b), v)
    d = jnp.exp(b_last[:, :, :, 0])

    def chunk_step(s, inp):
        u_n, d_n = inp
        return d_n[..., None] * s + u_n, s

    _, s_prev = lax.scan(chunk_step, jnp.zeros((B, H, K, V), jnp.float32),
                         (jnp.moveaxis(u, 2, 0), jnp.moveaxis(d, 2, 0)))
    s_prev = jnp.moveaxis(s_prev, 0, 2)
    o = o + jnp.einsum('bhntk,bhnkv->bhntv', q * jnp.exp(b), s_prev)
    return o.transpose(0, 2, 3, 1, 4).reshape(B, T, H, V)


def rwkv7_scan(r, w, k, v, a, b):
    B, T, H, N = r.shape

    def step(s, inp):
        r_t, w_t, k_t, v_t, a_t, b_t = inp
        sa = jnp.einsum('bhvk,bhk->bhv', s, a_t)
        s = s * w_t[:, :, None, :] + sa[..., None] * b_t[:, :, None, :] + v_t[..., None] * k_t[:, :, None, :]
        return s, jnp.einsum('bhvk,bhk->bhv', s, r_t)

    tm = lambda t: jnp.moveaxis(t, 1, 0)
    _, y = lax.scan(step, jnp.zeros((B, H, N, N), jnp.float32),
                    (tm(r), tm(w), tm(k), tm(v), tm(a), tm(b)))
    return jnp.moveaxis(y, 0, 1)


def hybrid_mixer(h, w_in, lb, hgrn_out_norm, mu, w0, w2, a0, a2, g2, k_k, k_a, r_k, gn_w, gn_b, w_out):
    B, T, _ = h.shape
    f32 = jnp.float32
    p = h @ w_in

    q_a, f_a, i_a, g_a = jnp.split(p[..., :N_HGRN_COLS].astype(f32), 4, axis=-1)
    forget = lb + (1.0 - lb) * jax.nn.sigmoid(f_a)
    heads_a = lambda t: t.reshape(B, T, HA_HEADS, HA_HEAD_DIM)
    o_a = hgrn2_chunkwise(heads_a(jax.nn.silu(q_a)), heads_a(1.0 - forget),
                          heads_a(i_a), heads_a(jnp.log(forget)))
    o_a = o_a * lax.rsqrt(jnp.mean(o_a * o_a, axis=-1, keepdims=True) + NORM_EPS)
    o_a = o_a.reshape(B, T, W_A) * hgrn_out_norm * jax.nn.silu(g_a)

    pr = p[..., N_HGRN_COLS:].astype(f32)
    pr = pr + mu * (token_shift(pr) - pr)
    splits = [W_B, 2 * W_B, 3 * W_B, 3 * W_B + DECAY_LORA, 3 * W_B + DECAY_LORA + AAA_LORA]
    r, k, v, w_low, a_low, g_low = jnp.split(pr, splits, axis=-1)
    w_log = -jax.nn.softplus(-(w0 + jnp.tanh(w_low) @ w2)) - 0.5
    decay = jnp.exp(-jnp.exp(w_log))
    a = jax.nn.sigmoid(a0 + a_low @ a2)
    g = jax.nn.sigmoid(g_low) @ g2
    heads_b = lambda t: t.reshape(B, T, HB_HEADS, HB_HEAD_DIM)
    kk = heads_b(k * k_k)
    kk = kk / jnp.maximum(jnp.sqrt(jnp.sum(kk * kk, axis=-1, keepdims=True)), L2_EPS)
    k = k * (1.0 + (a - 1.0) * k_a)
    r_h, k_h, v_h, a_h = heads_b(r), heads_b(k), heads_b(v), heads_b(a)
    y = rwkv7_scan(r_h, heads_b(decay), k_h, v_h, -kk, kk * a_h)
    mean = jnp.mean(y, axis=-1, keepdims=True)
    var = jnp.mean(jnp.square(y - mean), axis=-1, keepdims=True)
    y = ((y - mean) * lax.rsqrt(var + RWKV_GN_EPS)).reshape(B, T, W_B) * gn_w + gn_b
    bonus = (jnp.sum(r_h * k_h * r_k, axis=-1, keepdims=True) * v_h).reshape(B, T, W_B)
    o_b = (y + bonus) * g

    return jnp.concatenate([o_a, o_b], axis=-1).astype(h.dtype) @ w_out


def setup_inputs(seed: int = 0) -> dict:
    key = jax.random.key(seed)
    ks = iter(jax.random.split(key, 40))
    f32 = jnp.float32
    nrm = lambda shape, scale: jax.random.normal(next(ks), shape, f32) * scale
    uni = lambda shape, lo, hi: jax.random.uniform(next(ks), shape, f32, lo, hi)
    L = DEPTH
    return {
        'x': nrm((BATCH, SEQ, D_MODEL), 1.0),
        'ffn1_norm': 1.0 + nrm((L, D_MODEL), 0.02),
        'ffn1_w_gate': nrm((L, D_MODEL, D_FF), D_MODEL ** -0.5),
        'ffn1_w_up': nrm((L, D_MODEL, D_FF), D_MODEL ** -0.5),
        'ffn1_w_down': nrm((L, D_FF, D_MODEL), D_FF ** -0.5),
        'mix_norm': 1.0 + nrm((L, D_MODEL), 0.02),
        'w_in': nrm((L, D_MODEL, D_IN_PROJ), D_MODEL ** -0.5),
        'hgrn_lb_logits': nrm((L + 1, W_A), 0.1),
        'hgrn_out_norm': 1.0 + nrm((L, W_A), 0.02),
        'rwkv_shift_mu': uni((L, N_RWKV_COLS), 0.0, 1.0),
        'rwkv_w0': uni((L, W_B), -5.0, 1.0),
        'rwkv_w2': nrm((L, DECAY_LORA, W_B), 0.5 * DECAY_LORA ** -0.5),
        'rwkv_a0': nrm((L, W_B), 0.1),
        'rwkv_a2': nrm((L, AAA_LORA, W_B), 0.5 * AAA_LORA ** -0.5),
        'rwkv_g2': nrm((L, GATE_LORA, W_B), GATE_LORA ** -0.5),
        'rwkv_k_k': 0.85 + nrm((L, W_B), 0.05),
        'rwkv_k_a': 1.0 + nrm((L, W_B), 0.05),
        'rwkv_r_k': nrm((L, HB_HEADS, HB_HEAD_DIM), 0.1),
        'rwkv_gn_w': 1.0 + nrm((L, W_B), 0.02),
        'rwkv_gn_b': nrm((L, W_B), 0.02),
        'w_out': nrm((L, D_MIX, D_MODEL), D_MIX ** -0.5),
        'ffn2_norm': 1.0 + nrm((L, D_MODEL), 0.02),
        'ffn2_w_gate': nrm((L, D_MODEL, D_FF), D_MODEL ** -0.5),
        'ffn2_w_up': nrm((L, D_MODEL, D_FF), D_MODEL ** -0.5),
        'ffn2_w_down': nrm((L, D_FF, D_MODEL), D_FF ** -0.5),
        'final_norm': 1.0 + nrm((D_MODEL,), 0.02),
    }


def reference(x, ffn1_norm, ffn1_w_gate, ffn1_w_up, ffn1_w_down, mix_norm, w_in, hgrn_lb_logits,
              hgrn_out_norm, rwkv_shift_mu, rwkv_w0, rwkv_w2, rwkv_a0, rwkv_a2, rwkv_g2, rwkv_k_k,
              rwkv_k_a, rwkv_r_k, rwkv_gn_w, rwkv_gn_b, w_out, ffn2_norm, ffn2_w_gate, ffn2_w_up,
              ffn2_w_down, final_norm):
    lower_bounds = jnp.cumsum(jax.nn.softmax(hgrn_lb_logits.astype(jnp.float32), axis=0), axis=0)
    for l in range(DEPTH):
        x = x + 0.5 * swiglu(rms_norm(x, ffn1_norm[l]), ffn1_w_gate[l], ffn1_w_up[l], ffn1_w_down[l])
        x = x + hybrid_mixer(rms_norm(x, mix_norm[l]), w_in[l], lower_bounds[l], hgrn_out_norm[l],
                             rwkv_shift_mu[l], rwkv_w0[l], rwkv_w2[l], rwkv_a0[l], rwkv_a2[l],
                             rwkv_g2[l], rwkv_k_k[l], rwkv_k_a[l], rwkv_r_k[l], rwkv_gn_w[l],
                             rwkv_gn_b[l], w_out[l])
        x = x + 0.5 * swiglu(rms_norm(x, ffn2_norm[l]), ffn2_w_gate[l], ffn2_w_up[l], ffn2_w_down[l])
    return rms_norm(x, final_norm)
```

```python
from contextlib import ExitStack
import os
import numpy as np
import ml_dtypes
import concourse.bass as bass
import concourse.mybir as mybir
from concourse.bass_utils import run_bass_kernel_spmd

F32 = mybir.dt.float32
BF16 = mybir.dt.bfloat16
AF = mybir.ActivationFunctionType
ALU = mybir.AluOpType

D = 1024
DFF = 2816
TT = 512
NHC = 22
C0 = float(np.exp(-0.5))
NORM_EPS = 1e-6
GN_EPS = 64e-5
NPAN_IN = 17
STAGE = os.environ.get("MK_STAGE", "all")
CUT = int(os.environ.get("MK_CUT", "99"))

V_GF1, V_GMX, V_GF2, V_GFN = 0, 8, 16, 24
V_L0, V_L1, V_ON = 32, 36, 40
V_MUR, V_MUK, V_MUV, V_W0, V_A0, V_KK, V_KA, V_RK, V_GNW, V_GNB = 44, 52, 60, 68, 76, 84, 92, 100, 108, 116
V_MUWL, V_MUAL, V_MUGL = 124, 125, 126
NV = 128
K_ID, K_MI, K_MST, K_MS, K_RM = 0, 128, 640, 1152, 1664
NCST = 2176


class Prog:
    ENGS = ("pe", "act", "dve", "pool", "sp")

    def __init__(self, nc):
        self.nc = nc
        self.ops = {e: [] for e in self.ENGS}
        self.cnt = {}
        self.known = {e: {} for e in self.ENGS}
        self.st = {}
        self.es = ExitStack()
        self.rec = None

    def record(self, f, *a):
        self.rec = []
        f(*a)
        out, self.rec = self.rec, None
        return out

    def replay(self, la, lb=()):
        la, lb = list(la), list(lb)
        ia = ib = 0
        while ia < len(la) or ib < len(lb):
            if ib >= len(lb) or (ia < len(la) and ia * len(lb) <= ib * len(la)):
                o = la[ia]; ia += 1
            else:
                o = lb[ib]; ib += 1
            self.op(o[0], o[1], o[2], o[3], o[4])

    def sb(self, name, shape, dt):
        return self.es.enter_context(self.nc.sbuf_tensor("s_" + name, list(shape), dt))

    def ps(self, name, shape, dt=F32):
        return self.es.enter_context(self.nc.psum_tensor("p_" + name, list(shape), dt))

    def _need(self, eng, tok, waits):
        if tok is None:
            return
        sk, val = tok
        if self.known[eng].get(sk, 0) >= val:
            return
        if waits.get(sk, 0) < val:
            waits[sk] = val

    def op(self, eng, fn, r=(), w=(), dsem=None):
        if self.rec is not None:
            self.rec.append((eng, fn, tuple(r), tuple(w), dsem))
            return None
        own = "e_" + eng
        waits = {}
        for k in r:
            s = self.st.get(k)
            if s and s[0] is not None:
                if not (eng == "pe" and s[0][0] == own):
                    self._need(eng, s[0], waits)
        for k in w:
            s = self.st.get(k)
            if s:
                if s[0] is not None and s[0][0] != own:
                    self._need(eng, s[0], waits)
                for sk, v in s[1].items():
                    if sk != own:
                        self._need(eng, (sk, v), waits)
        for sk, v in waits.items():
            self.known[eng][sk] = v
        if dsem is None:
            sk, inc = own, 1
        else:
            sk, inc = "d_" + dsem, 16
        self.cnt[sk] = self.cnt.get(sk, 0) + inc
        tok = (sk, self.cnt[sk])
        self.ops[eng].append((list(waits.items()), fn, sk, inc))
        for k in r:
            s = self.st.setdefault(k, [None, {}])
            s[1][sk] = tok[1]
        for k in w:
            self.st[k] = [tok, {}]
        return tok

    def barrier(self):
        ce = ("pe", "act", "dve", "pool")
        for e in ce:
            waits = {}
            for o in ce:
                if o != e and self.cnt.get("e_" + o, 0) > 0:
                    self._need(e, ("e_" + o, self.cnt["e_" + o]), waits)
            for sk, v in waits.items():
                self.known[e][sk] = v
            if waits:
                self.ops[e].append((list(waits.items()), None, None, 0))

    def emit(self, final_waits=()):
        nc = self.nc
        with ExitStack() as es:
            sems = {sk: es.enter_context(nc.semaphore(sk)) for sk in sorted(self.cnt)}
            block = es.enter_context(nc.Block())

            def run(engname):
                def f(e):
                    for waits, fn, sk, inc in self.ops[engname]:
                        for wsk, v in waits:
                            e.wait_ge(sems[wsk], v)
                        if fn is not None:
                            fn(e).then_inc(sems[sk], inc)
                    for d, eng in final_waits:
                        if eng == engname:
                            e.wait_ge(sems["d_" + d], self.cnt["d_" + d])
                return f

            block.tensor(run("pe"))
            block.scalar(run("act"))
            block.vector(run("dve"))
            block.gpsimd(run("pool"))
            block.sync(run("sp"))


class Arena:
    def __init__(self, P, nbytes):
        self.t = P.sb("arena", [128, nbytes // 2], BF16)
        self.n = nbytes // 2
        self.off = 0
        self.uid = 0

    def reset(self):
        self.off = 0

    def alloc(self, free_shape, dt):
        n = int(np.prod(free_shape))
        n16 = n * (2 if dt == F32 else 1)
        if dt == F32 and self.off % 2:
            self.off += 1
        assert self.off + n16 <= self.n, ("arena overflow", self.off, n16, self.n)
        v = self.t[:, self.off:self.off + n16]
        self.off += n16
        if dt == F32:
            v = v.bitcast(F32)
        if len(free_shape) == 2:
            v = v.rearrange("p (a b) -> p a b", b=free_shape[1])
        elif len(free_shape) == 3:
            v = v.rearrange("p (a b c) -> p a b c", b=free_shape[1], c=free_shape[2])
        elif len(free_shape) == 4:
            v = v.rearrange("p (a b c d) -> p a b c d", b=free_shape[1], c=free_shape[2], d=free_shape[3])
        self.uid += 1
        return v, "ar%d" % self.uid


def build(T):
    NT = T // TT
    nc = bass.Bass("TRN2", target_bir_lowering=False)

    def din(name, shape, dt=F32):
        return nc.dram_tensor(name, list(shape), dt, kind="ExternalInput").ap()

    def dscr(name, shape, dt=BF16):
        return nc.dram_tensor(name, list(shape), dt, kind="Internal").ap()

    xT = din("xT", [128, 8, T])
    wf = {
        "gu1": din("wgu1", [11, 128, 4096]), "d1": din("wd1", [8, 128, 2816]),
        "in": din("win", [NPAN_IN, 128, 2048]), "out": din("wout", [4, 128, 3072]),
        "gu2": din("wgu2", [11, 128, 4096]), "d2": din("wd2", [8, 128, 2816]),
    }
    wb = {k: dscr("b_" + k, v.shape) for k, v in wf.items()}
    small_d = din("small", [128, 1536])
    vec_d = din("vec", [128, NV])
    cst_d = din("cst", [128, NCST], BF16)
    outT = nc.dram_tensor("outT", [128, 8, T], F32, kind="ExternalOutput").ap()

    P = Prog(nc)
    xt = [P.sb("xt0", [128, 8, TT], F32), P.sb("xt1", [128, 8, TT], F32)]
    hb = P.sb("hb", [128, 8, TT], BF16)
    NRING = 4
    wring = [P.sb("wr%d" % i, [128, 4096], BF16) for i in range(NRING)]
    cst = P.sb("cstb", [128, NCST], BF16)
    vec = P.sb("vec", [128, NV], F32)
    vec2 = P.sb("vec2", [128, NV], F32)
    lbv = P.sb("lbv", [128, 8], F32)
    smallf = P.sb("smallf", [128, 1536], F32)
    smallb = P.sb("smallb", [128, 1536], BF16)
    onesb = P.sb("onesb", [128, 128], BF16)
    epsb = P.sb("epsb", [128, 4], F32)
    rstd = P.sb("rstd", [128, TT], F32)
    oA = P.sb("oA", [128, 4, TT], BF16)
    oB = P.sb("oB", [64, 8, TT], BF16)
    S32 = P.sb("S32", [128, 4, 128], F32)
    H32 = P.sb("H32", [64, 8, 64], F32)
    carry = P.sb("carry", [128, 32], F32)
    lora_w = P.sb("lora_w", [32, TT], BF16)
    lora_a = P.sb("lora_a", [32, TT], BF16)
    lora_g = P.sb("lora_g", [96, TT], BF16)
    AR = Arena(P, 94 * 1024)
    psF = [P.ps("psF%d" % i, [128, 1024], F32) for i in range(3)]
    ptB = [P.ps("ptB%d" % i, [128, 1024], BF16) for i in range(2)]
    rr = {"h": 0, "d": 0, "t": 0, "w": 0}

    rr.update({"pool": None, "hA": 0, "hB": 0})

    def nxt():
        if rr["pool"] == "A":
            i = rr["hA"] % 3
            rr["hA"] += 1
        elif rr["pool"] == "B":
            i = 3 + rr["hB"] % 3
            rr["hB"] += 1
        else:
            i = rr["h"] % 6
            rr["h"] += 1
        return psF[i // 2][:, (i % 2) * 512:(i % 2 + 1) * 512], "psF%d_%d" % (i // 2, i % 2)

    def nxt2():
        if rr["pool"] == "A":
            i = 0
        elif rr["pool"] == "B":
            i = 2
        else:
            i = rr["d"] % 3
            rr["d"] += 1
        return psF[i], ["psF%d_0" % i, "psF%d_1" % i]

    def nxtT():
        if rr["pool"] == "A":
            i = 0
        elif rr["pool"] == "B":
            i = 1
        else:
            i = rr["t"] % 2
            rr["t"] += 1
        return ptB[i], "ptB%d" % i

    ident = cst[:, K_ID:K_ID + 128]
    maskI = cst[0:64, K_MI:K_MI + 512]
    maskST = cst[0:64, K_MST:K_MST + 512]
    maskS = cst[0:64, K_MS:K_MS + 512]
    rmask = cst[:, K_RM:K_RM + 512]

    def mm(out, lhsT, rhs, start, stop, r, w):
        P.op("pe", lambda e: e.matmul(out, lhsT=lhsT, rhs=rhs, start=start, stop=stop), r=r, w=w)

    def tr(out, in_, idn, r, w):
        P.op("pe", lambda e: e.transpose(out, in_, idn), r=r, w=w)

    def act(out, in_, func, r, w, bias=None, scale=None):
        kw = {}
        if bias is not None:
            kw["bias"] = bias
        if scale is not None:
            kw["scale"] = scale
        P.op("act", lambda e: e.activation(out=out, in_=in_, func=func, **kw), r=r, w=w)

    def cp(eng, out, in_, r, w):
        if eng == "act":
            P.op("act", lambda e: e.activation(out=out, in_=in_, func=AF.Copy), r=r, w=w)
        else:
            P.op(eng, lambda e: e.tensor_copy(out=out, in_=in_), r=r, w=w)

    def tt(eng, out, in0, in1, op, r, w):
        P.op(eng, lambda e: e.tensor_tensor(out=out, in0=in0, in1=in1, op=op), r=r, w=w)

    def ts(eng, out, in0, s1, s2, op0, op1, r, w):
        if s2 is None and eng == "pool" and op0 == ALU.mult:
            P.op(eng, lambda e: e.tensor_scalar(out=out, in0=in0, scalar1=s1, scalar2=0.0, op0=ALU.mult, op1=ALU.add), r=r, w=w)
        elif s2 is None:
            P.op(eng, lambda e: e.tensor_scalar(out=out, in0=in0, scalar1=s1, scalar2=None, op0=op0), r=r, w=w)
        else:
            P.op(eng, lambda e: e.tensor_scalar(out=out, in0=in0, scalar1=s1, scalar2=s2, op0=op0, op1=op1), r=r, w=w)

    def stt(out, in0, scalar, in1, op0, op1, r, w):
        P.op("dve", lambda e: e.scalar_tensor_tensor(out=out, in0=in0, scalar=scalar, in1=in1, op0=op0, op1=op1), r=r, w=w)

    def scan(out, d0, d1, r, w):
        P.op("dve", lambda e: e.tensor_tensor_scan(out=out, data0=d0, data1=d1, initial=0.0, op0=ALU.mult, op1=ALU.add), r=r, w=w)

    def dma(eng, out, in_, r, w, dsem):
        P.op(eng, lambda e: e.dma_start(out=out, in_=in_), r=r, w=w, dsem=dsem)

    dma("sp", cst[:], cst_d, [], ["cst"], "c0")
    dma("sp", vec[:], vec_d, [], ["vec"], "c1")
    dma("sp", smallf[:], small_d, [], ["smallf"], "c2")
    cast_keys = {}
    for k in ("gu1", "d1", "in", "out", "gu2", "d2"):
        cast_keys[k] = []
        npan = wf[k].shape[0]
        for pn in range(npan):
            key = "wb_%s_%d" % (k, pn)
            allk = [kk_ for v_ in cast_keys.values() for kk_ in v_]
            dma("pool", wb[k][pn], wf[k][pn], allk[-4:-3], [key], "cast%d" % (len(allk) % 4))
            cast_keys[k].append(key)
    cp("act", smallb[:], smallf[:], ["smallf"], ["smallb"])
    P.op("dve", lambda e: e.memset(onesb[:], 1.0), w=["onesb"])
    P.op("dve", lambda e: e.memset(epsb[:, 0:1], NORM_EPS), w=["epsb0"])
    P.op("dve", lambda e: e.memset(epsb[:, 1:2], GN_EPS), w=["epsb1"])
    P.op("dve", lambda e: e.memset(epsb[:, 2:3], 1.0), w=["epsb2"])
    P.op("dve", lambda e: e.memset(S32[:], 0.0), w=["S32_%d" % h for h in range(4)])
    P.op("dve", lambda e: e.memset(H32[:], 0.0), w=["H32_%d" % h for h in range(8)])
    P.op("dve", lambda e: e.memset(carry[:], 0.0), w=["carry"])
    ts("dve", vec2[:], vec[:], -1.0, 1.0, ALU.mult, ALU.add, ["vec"], ["vec2"])
    tt("dve", lbv[:, 0:4], vec[:, V_L0:V_L0 + 4], vec[:, V_L1:V_L1 + 4], ALU.subtract, ["vec"], ["lbv"])
    act(lbv[:, 0:4], lbv[:, 0:4], AF.Sigmoid, ["lbv"], ["lbv"])
    ts("dve", lbv[:, 4:8], lbv[:, 0:4], -1.0, 1.0, ALU.mult, ALU.add, ["lbv"], ["lbv2"])
    CONSTK = ["cst", "vec", "vec2", "lbv", "lbv2", "smallb", "onesb"]

    def load_panel(kind, pn, nelem):
        i = rr["w"] % NRING
        rr["w"] += 1
        key = "wr%d" % i
        dma("sp", wring[i][:, 0:nelem], wb[kind][pn], cast_keys[kind], [key], key)
        return wring[i], key

    def rmsnorm_hb(X, xkeys, gcol, sqb, sqk):
        act(sqb[:], X[:], AF.Square, xkeys, [sqk])
        bank, bk = nxt()
        for kc in range(8):
            mm(bank, onesb[:], sqb[:, kc, :], kc == 0, kc == 7, ["onesb", sqk], [bk])
        act(rstd[:], bank, AF.Ln, [bk, "epsb0"], ["rstd"], bias=epsb[:, 0:1], scale=1.0 / D)
        act(rstd[:], rstd[:], AF.Exp, ["rstd"], ["rstd"], scale=-0.5)
        for kc in range(8):
            stt(hb[:, kc, :], X[:, kc, :], vec[:, gcol + kc:gcol + kc + 1], rstd[:], ALU.mult, ALU.mult,
                [xkeys[kc], "rstd", "vec"], ["hb"])

    def ffn(X, xkeys, gcol, kgu, kd):
        P.barrier()
        AR.reset()
        sqb, sqk = AR.alloc([8, TT], BF16)
        hid, _ = AR.alloc([NHC, TT], BF16)
        sg = [AR.alloc([TT], F32) for _ in range(2)]
        rmsnorm_hb(X, xkeys, gcol, sqb, sqk)
        for pn in range(11):
            wt, wk = load_panel(kgu, pn, 4096)
            wv = wt[:].rearrange("p (g k c) -> p g k c", g=2, k=8)
            for half in range(2):
                m = pn * 2 + half
                bg, bgk = nxt()
                bu, buk = nxt()
                for kc in range(8):
                    mm(bg, wv[:, 0, kc, half * 128:(half + 1) * 128], hb[:, kc, :], kc == 0, kc == 7, [wk, "hb"], [bgk])
                for kc in range(8):
                    mm(bu, wv[:, 1, kc, half * 128:(half + 1) * 128], hb[:, kc, :], kc == 0, kc == 7, [wk, "hb"], [buk])
                sgv, sgk = sg[m % 2]
                act(sgv, bg, AF.Silu, [bgk], [sgk])
                tt("dve", hid[:, m, :], sgv, bu, ALU.mult, [sgk, buk], ["hid%d" % m])
        for mo in range(8):
            wt, wk = load_panel(kd, mo, 2816)
            wv = wt[:, 0:2816].rearrange("p (k c) -> p k c", k=NHC)
            by, byk = nxt()
            for m in range(NHC):
                mm(by, wv[:, m, :], hid[:, m, :], m == 0, m == NHC - 1, [wk, "hid%d" % m], [byk])
            stt(X[:, mo, :], by, 0.5, X[:, mo, :], ALU.mult, ALU.add, [byk, xkeys[mo]], [xkeys[mo]])

    def shift_lerp(bank, bk, R, ccol, mucol, out, outk, raws, tmpl):
        tmp, tk = tmpl
        ck = "carry%d" % ccol
        ts("pool", tmp[0:R, 0:1], carry[0:R, ccol:ccol + 1], vec[0:R, mucol:mucol + 1], None, ALU.mult, None,
           [ck, "carry", "vec"], [tk])
        P.op("act", lambda e: e.activation(out=tmp[0:R, 1:TT], in_=bank[0:R, 0:TT - 1], func=AF.Identity,
                                           scale=vec[0:R, mucol:mucol + 1]), r=[bk, "vec"], w=[tk])
        stt(out, bank[0:R, :], vec2[0:R, mucol:mucol + 1], tmp[0:R, :], ALU.mult, ALU.add, [bk, tk, "vec2"], [outk])
        cp("dve", carry[0:R, ccol:ccol + 1], bank[0:R, TT - 1:TT], [bk], [ck])

    def mixer(X, xkeys):
        P.barrier()
        AR.reset()
        sqb, sqk = AR.alloc([8, TT], BF16)
        rmsnorm_hb(X, xkeys, V_GMX, sqb, sqk)
        f32t = lambda: AR.alloc([TT], F32)
        b16t = lambda: AR.alloc([TT], BF16)
        HS = []
        for _s in range(2):
            d_ = {}
            for nm in ("sqt", "qs", "kf", "logf", "bb", "Dd", "E1", "E2", "t1", "gg"):
                d_[nm] = f32t()
            d_["dv"] = AR.alloc([8], F32)
            for nm in ("Qs", "Ks", "Qb", "KlT", "vf", "osq"):
                d_[nm] = b16t()
            d_["Vtok"] = AR.alloc([8, 128], BF16)
            d_["Kltok"] = AR.alloc([8, 128], BF16)
            d_["scT"] = AR.alloc([8, 64], BF16)
            d_["Sbf"] = AR.alloc([8, 128], BF16)
            HS.append(d_)
        def hgrn_head(h):
            d_ = HS[h % 2]
            (sqt, sqtk), (qs, qsk), (kf, kfk), (logf, logfk), (bb, bbk) = d_["sqt"], d_["qs"], d_["kf"], d_["logf"], d_["bb"]
            (Dd, Ddk), (E1, E1k), (E2, E2k), (t1, t1k), (gg, ggk), (dv, dvk) = d_["Dd"], d_["E1"], d_["E2"], d_["t1"], d_["gg"], d_["dv"]
            (Qs, Qsk), (Ks, Ksk), (Qb, Qbk), (KlT, KlTk), (vf, vfk), (osq, osqk) = d_["Qs"], d_["Ks"], d_["Qb"], d_["KlT"], d_["vf"], d_["osq"]
            (Vtok, Vtokk), (Kltok, Kltokk), (scT, scTk) = d_["Vtok"], d_["Kltok"], d_["scT"]
            Sbf = d_["Sbf"][0]
            sk_ = "S%dbf" % (h % 2)
            wA, wAk = load_panel("in", 2 * h, 2048)
            wAv = wA[:, 0:2048].rearrange("p (k c) -> p k c", k=8)
            bq, bqk = nxt()
            bf_, bfk = nxt()
            for kc in range(8):
                mm(bq, wAv[:, kc, 0:128], hb[:, kc, :], kc == 0, kc == 7, [wAk, "hb"], [bqk])
            for kc in range(8):
                mm(bf_, wAv[:, kc, 128:256], hb[:, kc, :], kc == 0, kc == 7, [wAk, "hb"], [bfk])
            act(sqt, bq, AF.Sigmoid, [bqk], [sqtk])
            tt("dve", qs, bq, sqt, ALU.mult, [bqk, sqtk], [qsk])
            act(kf, bf_, AF.Sigmoid, [bfk], [kfk], scale=-1.0)
            ts("dve", kf, kf, lbv[:, 4 + h:5 + h], None, ALU.mult, None, [kfk, "lbv2"], [kfk])
            wB, wBk = load_panel("in", 2 * h + 1, 2048)
            wBv = wB[:, 0:2048].rearrange("p (k c) -> p k c", k=8)
            bi, bik = nxt()
            bgg, bggk = nxt()
            for kc in range(8):
                mm(bi, wBv[:, kc, 0:128], hb[:, kc, :], kc == 0, kc == 7, [wBk, "hb"], [bik])
            for kc in range(8):
                mm(bgg, wBv[:, kc, 128:256], hb[:, kc, :], kc == 0, kc == 7, [wBk, "hb"], [bggk])
            cp("dve", vf, bi, [bik], [vfk])
            act(sqt, bgg, AF.Sigmoid, [bggk], [sqtk])
            tt("dve", gg, bgg, sqt, ALU.mult, [bggk, sqtk], [ggk])
            act(logf, kf, AF.Ln, [kfk, "epsb2"], [logfk], bias=epsb[:, 2:3], scale=-1.0)
            scan(bb, rmask, logf, [logfk, "cst"], [bbk])
            b3 = bb.rearrange("p (c t) -> p c t", t=64)
            D3 = Dd.rearrange("p (c t) -> p c t", t=64)
            tt("dve", D3, b3, b3[:, :, 32:33].broadcast_to([128, 8, 64]), ALU.subtract, [bbk], [Ddk])
            act(E1, Dd, AF.Exp, [Ddk], [E1k])
            act(E2, Dd, AF.Exp, [Ddk], [E2k], scale=-1.0)
            tt("pool", Qs, qs, E1, ALU.mult, [qsk, E1k], [Qsk])
            tt("dve", Ks, kf, E2, ALU.mult, [kfk, E2k], [Ksk])
            act(E1, bb, AF.Exp, [bbk], [E1k])
            tt("pool", Qb, qs, E1, ALU.mult, [qsk, E1k], [Qbk])
            tt("dve", D3, b3, b3[:, :, 63:64].broadcast_to([128, 8, 64]), ALU.subtract, [bbk], [Ddk])
            act(E2, Dd, AF.Exp, [Ddk], [E2k], scale=-1.0)
            tt("dve", KlT, kf, E2, ALU.mult, [kfk, E2k], [KlTk])
            act(dv, b3[:, :, 63], AF.Exp, [bbk], [dvk])
            tb, tbk = nxtT()
            for c in range(8):
                tr(tb[0:64, c * 128:(c + 1) * 128], vf[:, c * 64:(c + 1) * 64], ident, [vfk, "cst"], [tbk])
            cp("act", Vtok[0:64], tb[0:64, :].rearrange("p (c v) -> p c v", v=128), [tbk], [Vtokk])
            tb2, tb2k = nxtT()
            for c in range(8):
                tr(tb2[0:64, c * 128:(c + 1) * 128], KlT[:, c * 64:(c + 1) * 64], ident, [KlTk, "cst"], [tb2k])
            cp("dve", Kltok[0:64], tb2[0:64, :].rearrange("p (c v) -> p c v", v=128), [tb2k], [Kltokk])
            bs, bsk = nxt()
            for c in range(8):
                mm(bs[0:64, c * 64:(c + 1) * 64], Ks[:, c * 64:(c + 1) * 64], Qs[:, c * 64:(c + 1) * 64], True, True, [Ksk, Qsk], [bsk])
            tt("dve", scT[0:64], bs[0:64, :].rearrange("p (c t) -> p c t", t=64), maskI.rearrange("p (c t) -> p c t", t=64),
               ALU.mult, [bsk, "cst"], [scTk])
            bu2, bu2k = nxt2()
            for c in range(8):
                mm(bu2[:, c * 128:(c + 1) * 128], Kltok[0:64, c, :], Vtok[0:64, c, :], True, True, [Kltokk, Vtokk], [bu2k[c // 4]])
            cp("act", Sbf[:, 0, :], S32[:, h, :], ["S32_%d" % h], [sk_ + "0"])
            for c in range(8):
                stt(S32[:, h, :], S32[:, h, :], dv[:, c:c + 1], bu2[:, c * 128:(c + 1) * 128], ALU.mult, ALU.add,
                    ["S32_%d" % h, dvk, bu2k[c // 4]], ["S32_%d" % h])
                if c < 7:
                    cp("act" if c % 2 else "pool", Sbf[:, c + 1, :], S32[:, h, :], ["S32_%d" % h], [sk_ + str(c + 1)])
            bo, bok = nxt()
            for c in range(8):
                mm(bo[:, c * 64:(c + 1) * 64], Vtok[0:64, c, :], scT[0:64, c, :], True, False, [Vtokk, scTk], [bok])
                mm(bo[:, c * 64:(c + 1) * 64], Sbf[:, c, :], Qb[:, c * 64:(c + 1) * 64], False, True, [sk_ + str(c), Qbk], [bok])
            act(osq, bo, AF.Square, [bok], [osqk])
            bss, bssk = nxt()
            mm(bss, onesb[:], osq, True, True, ["onesb", osqk], [bssk])
            act(t1, bss, AF.Ln, [bssk, "epsb0"], [t1k], bias=epsb[:, 0:1], scale=1.0 / 128)
            act(t1, t1, AF.Exp, [t1k], [t1k], scale=-0.5)
            stt(t1, bo, vec[:, V_ON + h:V_ON + h + 1], t1, ALU.mult, ALU.mult, [bok, t1k, "vec"], [t1k])
            tt("dve", oA[:, h, :], t1, gg, ALU.mult, [t1k, ggk], ["oA%d" % h])

        for h0 in (0, 2):
            rr["pool"] = "A"
            la = P.record(hgrn_head, h0)
            rr["pool"] = "B"
            lb = P.record(hgrn_head, h0 + 1)
            rr["pool"] = None
            P.replay(la, lb)

        P.barrier()
        AR.reset()
        if STAGE == "hgrn" or CUT < 99:
            P.op("dve", lambda e: e.memset(oB[:], 0.0), w=["oB%d" % h for h in range(8)])
        raws = [AR.alloc([TT + 1], F32) for _ in range(2)]
        tmpl = AR.alloc([TT], F32)
        lt, ltk = AR.alloc([TT], F32)
        f32t = lambda: AR.alloc([TT], F32)
        b16t = lambda: AR.alloc([TT], BF16)
        r32, r32k = f32t()
        k32, k32k = f32t()
        v32, v32k = f32t()
        sgw, sgwk = f32t()
        asg, asgk = f32t()
        g32S = [f32t() for _ in range(2)]
        kkn, kknk = f32t()
        kmod, kmodk = f32t()
        cs, csk = f32t()
        Dd, Ddk = f32t()
        E1, E1k = f32t()
        E2, E2k = f32t()
        al, alk = f32t()
        bet, betk = f32t()
        bsumS = [f32t() for _ in range(2)]
        E1b, E1bk = f32t()
        y32, y32k = f32t()
        gamS = [AR.alloc([8], F32) for _ in range(2)]
        sq16, sq16k = b16t()
        sq16b, sq16bk = b16t()
        abar, abark = b16t()
        rbarS = [b16t() for _ in range(2)]
        Bh, Bhk = b16t()
        Kh, Khk = b16t()
        vb, vbk = b16t()
        ARflat, ARfk = AR.alloc([1024], BF16)
        BKflat, BKfk = AR.alloc([1024], BF16)
        ARf = ARflat.rearrange("p (c j t) -> p c j t", j=2, t=64)
        BKf = BKflat.rearrange("p (c j t) -> p c j t", j=2, t=64)
        tokbS = [[AR.alloc([8, 64], BF16) for _ in range(4)] for _ in range(2)]
        P0T, P0Tk = AR.alloc([8, 64], BF16)
        ArbTS = [AR.alloc([8, 64], BF16) for _ in range(2)]
        ArkTS = [AR.alloc([8, 64], BF16) for _ in range(2)]
        AakS = [AR.alloc([8, 64], BF16) for _ in range(2)]
        PmS0 = [AR.alloc([8, 64], BF16) for _ in range(2)]
        PTmS0 = [AR.alloc([8, 64], BF16) for _ in range(2)]
        TTmS0 = [AR.alloc([8, 64], BF16) for _ in range(2)]
        Pm1 = AR.alloc([8, 64], BF16)
        PTm1 = AR.alloc([8, 64], BF16)
        TTm1 = AR.alloc([8, 64], BF16)
        W1T, W1Tk = AR.alloc([8, 64], BF16)
        TAk, TAkk = AR.alloc([8, 64], BF16)
        Usb, _ = AR.alloc([8, 64], BF16)
        Hbf, _ = AR.alloc([8, 64], BF16)
        id64 = ident[0:64, 0:64]
        ones64 = onesb[0:64, 0:64]
        v3 = lambda a: a.rearrange("p (c t) -> p c t", t=64)

        wL, wLk = load_panel("in", 16, 2048)
        wLv = wL[:, 0:2048].rearrange("p (k c) -> p k c", k=8)
        for (c0_, c1_, R, ccol, mucol, dst, dstk, fn) in (
            (0, 32, 32, 24, V_MUWL, lora_w, "lora_w", AF.Tanh),
            (32, 64, 32, 25, V_MUAL, lora_a, "lora_a", AF.Copy),
            (64, 160, 96, 26, V_MUGL, lora_g, "lora_g", AF.Sigmoid),
        ):
            bk_, bkk = nxt()
            for kc in range(8):
                mm(bk_[0:R, :], wLv[:, kc, c0_:c1_], hb[:, kc, :], kc == 0, kc == 7, [wLk, "hb"], [bkk])
            shift_lerp(bk_, bkk, R, ccol, mucol, lt[0:R, :], ltk, raws[0], tmpl)
            act(dst[0:R, :], lt[0:R, :], fn, [ltk], [dstk])

        def stageA(h):
            sl = h % 2
            (rbar, rbark), (gam, gamk), (bsum, bsumk), (g32, g32k) = rbarS[sl], gamS[sl], bsumS[sl], g32S[sl]
            (ArbT, ArbTk), (ArkT, ArkTk), (Aak, Aakk) = ArbTS[sl], ArkTS[sl], AakS[sl]
            tokb = tokbS[sl]
            Pm = [PmS0[sl], Pm1]
            PTm = [PTmS0[sl], PTm1]
            TTm = [TTmS0[sl], TTm1]
            gi, e = h // 2, h % 2
            if e == 0:
                w1_, w1k = load_panel("in", 8 + 2 * gi, 2048)
                w2_, w2k = load_panel("in", 9 + 2 * gi, 2048)
                pw["v"] = (w1_[:, 0:2048].rearrange("p (k c) -> p k c", k=8), w1k,
                           w2_[:, 0:2048].rearrange("p (k c) -> p k c", k=8), w2k)
            w1v, w1k, w2v, w2k = pw["v"]
            for qi, (wv_, wk_, coff, mu0, dst, dstk) in enumerate((
                (w1v, w1k, e * 64, V_MUR, r32, r32k),
                (w1v, w1k, 128 + e * 64, V_MUK, k32, k32k),
                (w2v, w2k, e * 64, V_MUV, v32, v32k),
            )):
                bk_, bkk = nxt()
                for kc in range(8):
                    mm(bk_[0:64, :], wv_[:, kc, coff:coff + 64], hb[:, kc, :], kc == 0, kc == 7, [wk_, "hb"], [bkk])
                shift_lerp(bk_, bkk, 64, qi * 8 + h, mu0 + h, dst[0:64, :], dstk, raws[(qi + 1) % 2], tmpl)
            bz, bzk = nxt()
            mm(bz[0:64, :], smallb[0:32, h * 64:(h + 1) * 64], lora_w[:], True, True, ["smallb", "lora_w"], [bzk])
            act(sgw[0:64, :], bz[0:64, :], AF.Sigmoid, [bzk, "vec"], [sgwk], bias=vec[0:64, V_W0 + h:V_W0 + h + 1])
            bz, bzk = nxt()
            mm(bz[0:64, :], smallb[0:32, 512 + h * 64:512 + (h + 1) * 64], lora_a[:], True, True, ["smallb", "lora_a"], [bzk])
            act(asg[0:64, :], bz[0:64, :], AF.Sigmoid, [bzk, "vec"], [asgk], bias=vec[0:64, V_A0 + h:V_A0 + h + 1])
            bz, bzk = nxt()
            mm(bz[0:64, :], smallb[0:96, 1024 + h * 64:1024 + (h + 1) * 64], lora_g[:], True, True, ["smallb", "lora_g"], [bzk])
            cp("act", g32[0:64, :], bz[0:64, :], [bzk], [g32k])
            ts("pool", kkn[0:64, :], k32[0:64, :], vec[0:64, V_KK + h:V_KK + h + 1], None, ALU.mult, None, [k32k, "vec"], [kknk])
            tt("pool", sq16[0:64, :], kkn[0:64, :], kkn[0:64, :], ALU.mult, [kknk], [sq16k])
            bz, bzk = nxt()
            mm(bz[0:64, :], ones64, sq16[0:64, :], True, True, ["onesb", sq16k], [bzk])
            ts("dve", E1[0:64, :], bz[0:64, :], 1e-19, None, ALU.max, None, [bzk], [E1k])
            act(E1[0:64, :], E1[0:64, :], AF.Ln, [E1k], [E1k])
            act(E1[0:64, :], E1[0:64, :], AF.Exp, [E1k], [E1k], scale=-0.5)
            tt("dve", kkn[0:64, :], kkn[0:64, :], E1[0:64, :], ALU.mult, [kknk, E1k], [kknk])
            ts("dve", kmod[0:64, :], asg[0:64, :], -1.0, vec[0:64, V_KA + h:V_KA + h + 1], ALU.add, ALU.mult, [asgk, "vec"], [kmodk])
            stt(kmod[0:64, :], kmod[0:64, :], 1.0, k32[0:64, :], ALU.add, ALU.mult, [kmodk, k32k], [kmodk])
            tt("pool", E2[0:64, :], r32[0:64, :], kmod[0:64, :], ALU.mult, [r32k, kmodk], [E2k])
            ts("pool", sq16[0:64, :], E2[0:64, :], vec[0:64, V_RK + h:V_RK + h + 1], None, ALU.mult, None, [E2k, "vec"], [sq16k])
            bz, bzk = nxt()
            mm(bz[0:64, :], ones64, sq16[0:64, :], True, True, ["onesb", sq16k], [bzk])
            tt("dve", bsum[0:64, :], bz[0:64, :], v32[0:64, :], ALU.mult, [bzk, v32k], [bsumk])
            tt("pool", bet[0:64, :], kkn[0:64, :], asg[0:64, :], ALU.mult, [kknk, asgk], [betk])
            scan(cs[0:64, :], rmask[0:64, :], sgw[0:64, :], [sgwk, "cst"], [csk])
            cs3 = v3(cs[0:64, :])
            D3 = v3(Dd[0:64, :])
            act(E1[0:64, :], sgw[0:64, :], AF.Exp, [sgwk], [E1k], scale=C0)
            tt("dve", al[0:64, :], kkn[0:64, :], E1[0:64, :], ALU.mult, [kknk, E1k], [alk])
            tt("dve", D3, cs3, cs3[:, :, 32:33].broadcast_to([64, 8, 64]), ALU.subtract, [csk], [Ddk])
            act(E1[0:64, :], Dd[0:64, :], AF.Exp, [Ddk], [E1k], scale=-C0)
            act(E2[0:64, :], Dd[0:64, :], AF.Exp, [Ddk], [E2k], scale=C0)
            stt(ARf[0:64, :, 0, :], v3(al[0:64, :]), -1.0, v3(E1[0:64, :]), ALU.mult, ALU.mult, [alk, E1k], [ARfk])
            tt("pool", ARf[0:64, :, 1, :], v3(r32[0:64, :]), v3(E1[0:64, :]), ALU.mult, [r32k, E1k], [ARfk])
            tt("dve", BKf[0:64, :, 0, :], v3(bet[0:64, :]), v3(E2[0:64, :]), ALU.mult, [betk, E2k], [BKfk])
            tt("pool", BKf[0:64, :, 1, :], v3(kmod[0:64, :]), v3(E2[0:64, :]), ALU.mult, [kmodk, E2k], [BKfk])
            act(E1[0:64, :], cs[0:64, :], AF.Exp, [csk], [E1k], scale=-C0)
            stt(abar[0:64, :], al[0:64, :], -1.0, E1[0:64, :], ALU.mult, ALU.mult, [alk, E1k], [abark])
            tt("pool", rbar[0:64, :], r32[0:64, :], E1[0:64, :], ALU.mult, [r32k, E1k], [rbark])
            tt("dve", D3, cs3, cs3[:, :, 63:64].broadcast_to([64, 8, 64]), ALU.subtract, [csk], [Ddk])
            act(E2[0:64, :], Dd[0:64, :], AF.Exp, [Ddk], [E2k], scale=C0)
            tt("dve", Bh[0:64, :], bet[0:64, :], E2[0:64, :], ALU.mult, [betk, E2k], [Bhk])
            tt("pool", Kh[0:64, :], kmod[0:64, :], E2[0:64, :], ALU.mult, [kmodk, E2k], [Khk])
            act(gam[0:64, :], cs3[:, :, 63], AF.Exp, [csk], [gamk], scale=-C0)
            cp("pool", vb[0:64, :], v32[0:64, :], [v32k], [vbk])
            for qi, (src, srck) in enumerate(((abar, abark), (Bh, Bhk), (Kh, Khk), (vb, vbk))):
                tb, tbk = nxtT()
                for c in range(8):
                    tr(tb[0:64, c * 64:(c + 1) * 64], src[0:64, c * 64:(c + 1) * 64], id64, [srck, "cst"], [tbk])
                cp("act" if qi % 2 else "dve", tokb[qi][0][0:64], tb[0:64, 0:512].rearrange("p (c k) -> p c k", k=64), [tbk], [tokb[qi][1]])
            (Atok, Atokk), (Btok, Btokk), (Ktok, Ktokk), (Vtk, Vtkk) = tokb
            m1, m1k = nxt2()
            for c in range(8):
                mm(m1[0:64, c * 128:(c + 1) * 128], BKf[0:64, c, 0, :], ARflat[0:64, c * 128:(c + 1) * 128], True, True, [BKfk, ARfk], [m1k[c // 4]])
            m13 = m1[0:64, :].rearrange("p (c j t) -> p c j t", j=2, t=64)
            mI3 = maskI.rearrange("p (c t) -> p c t", t=64)
            mST3 = maskST.rearrange("p (c t) -> p c t", t=64)
            mS3 = maskS.rearrange("p (c t) -> p c t", t=64)
            tt("dve", P0T[0:64], m13[:, :, 0, :], mST3, ALU.mult, m1k + ["cst"], [P0Tk])
            tt("dve", ArbT[0:64], m13[:, :, 1, :], mI3, ALU.mult, m1k + ["cst"], [ArbTk])
            m2, m2k = nxt()
            for c in range(8):
                mm(m2[0:64, c * 64:(c + 1) * 64], BKf[0:64, c, 1, :], ARf[0:64, c, 1, :], True, True, [BKfk, ARfk], [m2k])
            tt("dve", ArkT[0:64], m2[0:64, :].rearrange("p (c t) -> p c t", t=64), mI3, ALU.mult, [m2k, "cst"], [ArkTk])
            m3, m3k = nxt2()
            for c in range(8):
                mm(m3[0:64, c * 128:(c + 1) * 128], ARf[0:64, c, 0, :], BKflat[0:64, c * 128:(c + 1) * 128], True, True, [BKfk, ARfk], [m3k[c // 4]])
            m33 = m3[0:64, :].rearrange("p (c j t) -> p c j t", j=2, t=64)
            tt("dve", Pm[0][0][0:64], m33[:, :, 0, :], mS3, ALU.mult, m3k + ["cst"], [Pm[0][1]])
            tt("dve", Aak[0:64], m33[:, :, 1, :], mS3, ALU.mult, m3k + ["cst"], [Aakk])
            cp("pool", PTm[0][0][0:64], P0T[0:64], [P0Tk], [PTm[0][1]])
            tt("pool", TTm[0][0][0:64], P0T[0:64], id64.rearrange("p (o t) -> p o t", o=1).broadcast_to([64, 8, 64]), ALU.add,
               [P0Tk, "cst"], [TTm[0][1]])
        def stageB(h):
            sl = h % 2
            (rbar, rbark), (gam, gamk), (bsum, bsumk), (g32, g32k) = rbarS[sl], gamS[sl], bsumS[sl], g32S[sl]
            (ArbT, ArbTk), (ArkT, ArkTk), (Aak, Aakk) = ArbTS[sl], ArkTS[sl], AakS[sl]
            tokb = tokbS[sl]
            Pm = [PmS0[sl], Pm1]
            PTm = [PTmS0[sl], PTm1]
            TTm = [TTmS0[sl], TTm1]
            (Atok, Atokk), (Btok, Btokk), (Ktok, Ktokk), (Vtk, Vtkk) = tokb
            (E1, E1k), (sq16, sq16k) = (E1b, E1bk), (sq16b, sq16bk)
            for j in range(1, 6):
                (Pc, Pck), (Pn, Pnk) = Pm[(j - 1) % 2], Pm[j % 2]
                (PTc, PTck), (PTn, PTnk) = PTm[(j - 1) % 2], PTm[j % 2]
                (Tc, Tck), (Tn, Tnk) = TTm[(j - 1) % 2], TTm[j % 2]
                bp, bpk = nxt()
                for c in range(8):
                    mm(bp[0:64, c * 64:(c + 1) * 64], PTc[0:64, c, :], Pc[0:64, c, :], True, True, [PTck, Pck], [bpk])
                cp("act", Pn[0:64], bp[0:64, :].rearrange("p (c t) -> p c t", t=64), [bpk], [Pnk])
                if j < 5:
                    bq_, bqk_ = nxt()
                    for c in range(8):
                        mm(bq_[0:64, c * 64:(c + 1) * 64], Pc[0:64, c, :], PTc[0:64, c, :], True, True, [PTck, Pck], [bqk_])
                    cp("dve", PTn[0:64], bq_[0:64, :].rearrange("p (c t) -> p c t", t=64), [bqk_], [PTnk])
                bt_, btk_ = nxt()
                for c in range(8):
                    mm(bt_[0:64, c * 64:(c + 1) * 64], Pn[0:64, c, :], Tc[0:64, c, :], True, True, [Pnk, Tck], [btk_])
                tt("dve", Tn[0:64], bt_[0:64, :].rearrange("p (c t) -> p c t", t=64), Tc[0:64], ALU.add, [btk_, Tck], [Tnk])
            Tf, Tfk = TTm[5 % 2]
            bw_, bwk_ = nxt()
            for c in range(8):
                mm(bw_[0:64, c * 64:(c + 1) * 64], Atok[0:64, c, :], Tf[0:64, c, :], True, True, [Atokk, Tfk], [bwk_])
            cp("act", W1T[0:64], bw_[0:64, :].rearrange("p (c t) -> p c t", t=64), [bwk_], [W1Tk])
            bw_, bwk_ = nxt()
            for c in range(8):
                mm(bw_[0:64, c * 64:(c + 1) * 64], Aak[0:64, c, :], Tf[0:64, c, :], True, True, [Aakk, Tfk], [bwk_])
            cp("dve", TAk[0:64], bw_[0:64, :].rearrange("p (c t) -> p c t", t=64), [bwk_], [TAkk])
            hk = "H32_%d" % h
            cp("act", Hbf[0:64, 0, :], H32[:, h, :], [hk], ["Hbf0"])
            for c in range(8):
                bU, bUk = nxt()
                mm(bU[0:64, 0:64], TAk[0:64, c, :], Vtk[0:64, c, :], True, False, [TAkk, Vtkk], [bUk])
                mm(bU[0:64, 0:64], W1T[0:64, c, :], Hbf[0:64, c, :], False, True, [W1Tk, "Hbf%d" % c], [bUk])
                cp("act", Usb[0:64, c, :], bU[0:64, 0:64], [bUk], ["Usb%d" % c])
                bH, bHk = nxt()
                mm(bH[0:64, 0:64], Btok[0:64, c, :], Usb[0:64, c, :], True, False, [Btokk, "Usb%d" % c], [bHk])
                mm(bH[0:64, 0:64], Ktok[0:64, c, :], Vtk[0:64, c, :], False, True, [Ktokk, Vtkk], [bHk])
                if c < 7:
                    stt(Hbf[0:64, c + 1, :], H32[:, h, :], gam[0:64, c:c + 1], bH[0:64, 0:64], ALU.mult, ALU.add,
                        [hk, gamk, bHk], ["Hbf%d" % (c + 1)])
                stt(H32[:, h, :], H32[:, h, :], gam[0:64, c:c + 1], bH[0:64, 0:64], ALU.mult, ALU.add, [hk, gamk, bHk], [hk])
            bY, bYk = nxt()
            for c in range(8):
                o_ = bY[0:64, c * 64:(c + 1) * 64]
                mm(o_, Hbf[0:64, c, :], rbar[0:64, c * 64:(c + 1) * 64], True, False, ["Hbf%d" % c, rbark], [bYk])
                mm(o_, Usb[0:64, c, :], ArbT[0:64, c, :], False, False, ["Usb%d" % c, ArbTk], [bYk])
                mm(o_, Vtk[0:64, c, :], ArkT[0:64, c, :], False, True, [Vtkk, ArkTk], [bYk])
            GN = int(os.environ.get("MK_GN", "99"))
            cp("act", y32[0:64, :], bY[0:64, :], [bYk], [y32k])
            cp("dve", sq16[0:64, :], y32[0:64, :], [y32k], [sq16k])
            bz, bzk = nxt()
            mm(bz[0:64, :], ones64, sq16[0:64, :], True, True, ["onesb", sq16k], [bzk])
            if GN >= 1:
                stt(y32[0:64, :], bz[0:64, :], -1.0 / 64, y32[0:64, :], ALU.mult, ALU.add, [bzk, y32k], [y32k])
                act(sq16[0:64, :], y32[0:64, :], AF.Square, [y32k], [sq16k])
                bz, bzk = nxt()
                mm(bz[0:64, :], ones64, sq16[0:64, :], True, True, ["onesb", sq16k], [bzk])
            if GN >= 2:
                act(E1[0:64, :], bz[0:64, :], AF.Ln, [bzk, "epsb1"], [E1k], bias=epsb[0:64, 1:2], scale=1.0 / 64)
                act(E1[0:64, :], E1[0:64, :], AF.Exp, [E1k], [E1k], scale=-0.5)
                tt("dve", y32[0:64, :], y32[0:64, :], E1[0:64, :], ALU.mult, [y32k, E1k], [y32k])
            if GN >= 3:
                ts("dve", y32[0:64, :], y32[0:64, :], vec[0:64, V_GNW + h:V_GNW + h + 1], vec[0:64, V_GNB + h:V_GNB + h + 1],
                   ALU.mult, ALU.add, [y32k, "vec"], [y32k])
            if GN >= 4:
                tt("pool", y32[0:64, :], y32[0:64, :], bsum[0:64, :], ALU.add, [y32k, bsumk], [y32k])
            if GN >= 5:
                tt("dve", oB[:, h, :], y32[0:64, :], g32[0:64, :], ALU.mult, [y32k, g32k], ["oB%d" % h])

        pw = {}
        NH = 8 if STAGE != "hgrn" else 0

        def recA(h):
            rr["pool"] = "A"
            l = P.record(stageA, h)
            rr["pool"] = None
            return l

        def recB(h):
            rr["pool"] = "B"
            l = P.record(stageB, h)
            rr["pool"] = None
            return l

        if NH:
            P.replay(recA(0))
        for h in range(NH):
            la = recA(h + 1) if h + 1 < NH else []
            lb = recB(h)
            P.replay(lb, la)

        for pn in range(4):
            wt, wk = load_panel("out", pn, 3072)
            wv = wt[:, 0:3072].rearrange("p (k c) -> p k c", k=12)
            for half in range(2):
                mo = pn * 2 + half
                bo, bok = nxt()
                for kc in range(4):
                    mm(bo, wv[:, kc, half * 128:(half + 1) * 128], oA[:, kc, :], kc == 0, False, [wk, "oA%d" % kc], [bok])
                for h in range(8):
                    mm(bo, wv[0:64, 4 + h, half * 128:(half + 1) * 128], oB[:, h, :], False, h == 7, [wk, "oB%d" % h], [bok])
                tt("dve", X[:, mo, :], bo, X[:, mo, :], ALU.add, [bok, xkeys[mo]], [xkeys[mo]])

    dma("sp", xt[0][:], xT[:, :, 0:TT], [], ["x0_%d" % k for k in range(8)], "x0")
    for it in range(NT):
        cur = it % 2
        X = xt[cur]
        xkeys = ["x%d_%d" % (cur, k) for k in range(8)]
        if it + 1 < NT:
            nx = 1 - cur
            dma("sp", xt[nx][:], xT[:, :, (it + 1) * TT:(it + 2) * TT], [], ["x%d_%d" % (nx, k) for k in range(8)], "x%d" % nx)
        ffn(X, xkeys, V_GF1, "gu1", "d1")
        if STAGE != "ffn1":
            mixer(X, xkeys)
            ffn(X, xkeys, V_GF2, "gu2", "d2")
        P.barrier()
        AR.reset()
        sqb, sqk = AR.alloc([8, TT], BF16)
        act(sqb[:], X[:], AF.Square, xkeys, [sqk])
        bank, bk = nxt()
        for kc in range(8):
            mm(bank, onesb[:], sqb[:, kc, :], kc == 0, kc == 7, ["onesb", sqk], [bk])
        act(rstd[:], bank, AF.Ln, [bk, "epsb0"], ["rstd"], bias=epsb[:, 0:1], scale=1.0 / D)
        act(rstd[:], rstd[:], AF.Exp, ["rstd"], ["rstd"], scale=-0.5)
        for kc in range(8):
            stt(X[:, kc, :], X[:, kc, :], vec[:, V_GFN + kc:V_GFN + kc + 1], rstd[:], ALU.mult, ALU.mult,
                [xkeys[kc], "rstd", "vec"], [xkeys[kc]])
        dma("pool", outT[:, :, it * TT:(it + 1) * TT], X[:], xkeys, [], "st")

    P.emit(final_waits=[("st", "pool")])
    P.es.close()
    return nc


def _host_layout(inp):
    f = np.float32
    def panels_gu(wg, wu):
        a = np.stack([wg, wu], 0).reshape(2, 8, 128, 11, 256)
        return np.ascontiguousarray(a.transpose(3, 2, 0, 1, 4)).reshape(11, 128, 4096)
    def panels_d(wd):
        a = wd.reshape(NHC, 128, 8, 128)
        return np.ascontiguousarray(a.transpose(2, 1, 0, 3)).reshape(8, 128, 2816)
    w_in = inp["w_in"][0]
    cols = []
    for h in range(4):
        cols += list(range(h * 128, (h + 1) * 128)) + list(range(512 + h * 128, 512 + (h + 1) * 128))
        cols += list(range(1024 + h * 128, 1024 + (h + 1) * 128)) + list(range(1536 + h * 128, 1536 + (h + 1) * 128))
    for gi in range(4):
        cols += list(range(2048 + gi * 128, 2048 + (gi + 1) * 128)) + list(range(2560 + gi * 128, 2560 + (gi + 1) * 128))
        cols += list(range(3072 + gi * 128, 3072 + (gi + 1) * 128)) + [-1] * 128
    cols += list(range(3584, 3744)) + [-1] * 96
    cols = np.array(cols)
    assert cols.size == NPAN_IN * 256
    wpad = np.concatenate([w_in, np.zeros((D, 1), f)], axis=1)
    wperm = wpad[:, cols]
    a = wperm.reshape(8, 128, NPAN_IN, 256)
    win = np.ascontiguousarray(a.transpose(2, 1, 0, 3)).reshape(NPAN_IN, 128, 2048)
    w_out = inp["w_out"][0]
    wo = np.zeros((12, 128, D), f)
    wo[0:4] = w_out[0:512].reshape(4, 128, D)
    wo[4:12, 0:64] = w_out[512:1024].reshape(8, 64, D)
    a = wo.reshape(12, 128, 4, 256)
    wout = np.ascontiguousarray(a.transpose(2, 1, 0, 3)).reshape(4, 128, 3072)
    small = np.zeros((128, 1536), f)
    small[0:32, 0:512] = inp["rwkv_w2"][0]
    small[0:32, 512:1024] = inp["rwkv_a2"][0]
    small[0:96, 1024:1536] = inp["rwkv_g2"][0]
    vec = np.zeros((128, NV), f)
    def put_d(c0, v):
        vec[:, c0:c0 + 8] = v.reshape(8, 128).T
    put_d(V_GF1, inp["ffn1_norm"][0]); put_d(V_GMX, inp["mix_norm"][0])
    put_d(V_GF2, inp["ffn2_norm"][0]); put_d(V_GFN, inp["final_norm"])
    vec[:, V_L0:V_L0 + 4] = inp["hgrn_lb_logits"][0].reshape(4, 128).T
    vec[:, V_L1:V_L1 + 4] = inp["hgrn_lb_logits"][1].reshape(4, 128).T
    vec[:, V_ON:V_ON + 4] = inp["hgrn_out_norm"][0].reshape(4, 128).T
    mu = inp["rwkv_shift_mu"][0]
    def put_h(c0, v):
        vec[0:64, c0:c0 + 8] = v.reshape(8, 64).T
    put_h(V_MUR, mu[0:512]); put_h(V_MUK, mu[512:1024]); put_h(V_MUV, mu[1024:1536])
    put_h(V_W0, inp["rwkv_w0"][0]); put_h(V_A0, inp["rwkv_a0"][0]); put_h(V_KK, inp["rwkv_k_k"][0])
    put_h(V_KA, inp["rwkv_k_a"][0]); put_h(V_RK, inp["rwkv_r_k"][0].reshape(-1))
    put_h(V_GNW, inp["rwkv_gn_w"][0]); put_h(V_GNB, inp["rwkv_gn_b"][0])
    vec[0:32, V_MUWL] = mu[1536:1568]; vec[0:32, V_MUAL] = mu[1568:1600]; vec[0:96, V_MUGL] = mu[1600:1696]
    cst = np.zeros((128, NCST), f)
    cst[:, K_ID:K_ID + 128] = np.eye(128, dtype=f)
    s = np.arange(64)[:, None]; t = np.arange(64)[None, :]
    cst[0:64, K_MI:K_MI + 512] = np.tile((s <= t).astype(f), (1, 8))
    cst[0:64, K_MST:K_MST + 512] = np.tile((s < t).astype(f), (1, 8))
    cst[0:64, K_MS:K_MS + 512] = np.tile((s > t).astype(f), (1, 8))
    rm = np.ones((128, 512), f); rm[:, ::64] = 0
    cst[:, K_RM:K_RM + 512] = rm
    shared = {
        "wgu1": panels_gu(inp["ffn1_w_gate"][0], inp["ffn1_w_up"][0]), "wd1": panels_d(inp["ffn1_w_down"][0]),
        "win": win, "wout": wout,
        "wgu2": panels_gu(inp["ffn2_w_gate"][0], inp["ffn2_w_up"][0]), "wd2": panels_d(inp["ffn2_w_down"][0]),
        "small": small, "vec": vec, "cst": cst.astype(ml_dtypes.bfloat16),
    }
    return shared


def _x_layout(xb):
    T = xb.shape[0]
    return np.ascontiguousarray(xb.reshape(T, 8, 128).transpose(2, 1, 0))


_NC_CACHE = {}


def kernel(**inputs):
    inp = {k: np.asarray(v) for k, v in inputs.items()}
    x = inp["x"]
    B, T, _ = x.shape
    shared = _host_layout(inp)
    if T not in _NC_CACHE:
        _NC_CACHE[T] = build(T)
    nc = _NC_CACHE[T]
    in_maps = []
    for b in range(B):
        m = dict(shared)
        m["xT"] = _x_layout(x[b])
        in_maps.append(m)
    res = run_bass_kernel_spmd(nc, in_maps, core_ids=list(range(B)))
    out = np.empty((B, T, D), np.float32)
    for b in range(B):
        o = np.asarray(res.results[b]["outT"])
        out[b] = o.transpose(2, 1, 0).reshape(T, D)
    return out
```

```python
from contextlib import ExitStack
import os
import numpy as np
import ml_dtypes
import concourse.bass as bass
import concourse.mybir as mybir
from concourse.bass_utils import run_bass_kernel_spmd

F32 = mybir.dt.float32
BF16 = mybir.dt.bfloat16
AF = mybir.ActivationFunctionType
ALU = mybir.AluOpType

D = 1024
DFF = 2816
TT = 512
NHC = 22
C0 = float(np.exp(-0.5))
NORM_EPS = 1e-6
GN_EPS = 64e-5
NPAN_IN = 17
STAGE = os.environ.get("MK_STAGE", "all")
CUT = int(os.environ.get("MK_CUT", "99"))

V_GF1, V_GMX, V_GF2, V_GFN = 0, 8, 16, 24
V_L0, V_L1, V_ON = 32, 36, 40
V_MUR, V_MUK, V_MUV, V_W0, V_A0, V_KK, V_KA, V_RK, V_GNW, V_GNB = 44, 52, 60, 68, 76, 84, 92, 100, 108, 116
V_MUWL, V_MUAL, V_MUGL = 124, 125, 126
NV = 128
K_ID, K_MI, K_MST, K_MS, K_RM = 0, 128, 640, 1152, 1664
NCST = 2176


class Prog:
    ENGS = ("pe", "act", "dve", "pool", "sp")

    def __init__(self, nc):
        self.nc = nc
        self.ops = {e: [] for e in self.ENGS}
        self.cnt = {}
        self.known = {e: {} for e in self.ENGS}
        self.st = {}
        self.es = ExitStack()
        self.rec = None

    def record(self, f, *a):
        self.rec = []
        f(*a)
        out, self.rec = self.rec, None
        return out

    def replay(self, la, lb=()):
        la, lb = list(la), list(lb)
        ia = ib = 0
        while ia < len(la) or ib < len(lb):
            if ib >= len(lb) or (ia < len(la) and ia * len(lb) <= ib * len(la)):
                o = la[ia]; ia += 1
            else:
                o = lb[ib]; ib += 1
            self.op(o[0], o[1], o[2], o[3], o[4])

    def sb(self, name, shape, dt):
        return self.es.enter_context(self.nc.sbuf_tensor("s_" + name, list(shape), dt))

    def ps(self, name, shape, dt=F32):
        return self.es.enter_context(self.nc.psum_tensor("p_" + name, list(shape), dt))

    def _need(self, eng, tok, waits):
        if tok is None:
            return
        sk, val = tok
        if self.known[eng].get(sk, 0) >= val:
            return
        if waits.get(sk, 0) < val:
            waits[sk] = val

    def op(self, eng, fn, r=(), w=(), dsem=None):
        if self.rec is not None:
            self.rec.append((eng, fn, tuple(r), tuple(w), dsem))
            return None
        own = "e_" + eng
        waits = {}
        for k in r:
            s = self.st.get(k)
            if s and s[0] is not None:
                if not (eng == "pe" and s[0][0] == own):
                    self._need(eng, s[0], waits)
        for k in w:
            s = self.st.get(k)
            if s:
                if s[0] is not None and s[0][0] != own:
                    self._need(eng, s[0], waits)
                for sk, v in s[1].items():
                    if sk != own:
                        self._need(eng, (sk, v), waits)
        for sk, v in waits.items():
            self.known[eng][sk] = v
        if dsem is None:
            sk, inc = own, 1
        else:
            sk, inc = "d_" + dsem, 16
        self.cnt[sk] = self.cnt.get(sk, 0) + inc
        tok = (sk, self.cnt[sk])
        self.ops[eng].append((list(waits.items()), fn, sk, inc))
        for k in r:
            s = self.st.setdefault(k, [None, {}])
            s[1][sk] = tok[1]
        for k in w:
            self.st[k] = [tok, {}]
        return tok

    def barrier(self):
        ce = ("pe", "act", "dve", "pool")
        for e in ce:
            waits = {}
            for o in ce:
                if o != e and self.cnt.get("e_" + o, 0) > 0:
                    self._need(e, ("e_" + o, self.cnt["e_" + o]), waits)
            for sk, v in waits.items():
                self.known[e][sk] = v
            if waits:
                self.ops[e].append((list(waits.items()), None, None, 0))

    def emit(self, final_waits=()):
        nc = self.nc
        with ExitStack() as es:
            sems = {sk: es.enter_context(nc.semaphore(sk)) for sk in sorted(self.cnt)}
            block = es.enter_context(nc.Block())

            def run(engname):
                def f(e):
                    for waits, fn, sk, inc in self.ops[engname]:
                        for wsk, v in waits:
                            e.wait_ge(sems[wsk], v)
                        if fn is not None:
                            fn(e).then_inc(sems[sk], inc)
                    for d, eng in final_waits:
                        if eng == engname:
                            e.wait_ge(sems["d_" + d], self.cnt["d_" + d])
                return f

            block.tensor(run("pe"))
            block.scalar(run("act"))
            block.vector(run("dve"))
            block.gpsimd(run("pool"))
            block.sync(run("sp"))


class Arena:
    def __init__(self, P, nbytes):
        self.t = P.sb("arena", [128, nbytes // 2], BF16)
        self.n = nbytes // 2
        self.off = 0
        self.uid = 0

    def reset(self):
        self.off = 0

    def alloc(self, free_shape, dt):
        n = int(np.prod(free_shape))
        n16 = n * (2 if dt == F32 else 1)
        if dt == F32 and self.off % 2:
            self.off += 1
        assert self.off + n16 <= self.n, ("arena overflow", self.off, n16, self.n)
        v = self.t[:, self.off:self.off + n16]
        self.off += n16
        if dt == F32:
            v = v.bitcast(F32)
        if len(free_shape) == 2:
            v = v.rearrange("p (a b) -> p a b", b=free_shape[1])
        elif len(free_shape) == 3:
            v = v.rearrange("p (a b c) -> p a b c", b=free_shape[1], c=free_shape[2])
        elif len(free_shape) == 4:
            v = v.rearrange("p (a b c d) -> p a b c d", b=free_shape[1], c=free_shape[2], d=free_shape[3])
        self.uid += 1
        return v, "ar%d" % self.uid


def build(T):
    NT = T // TT
    nc = bass.Bass("TRN2", target_bir_lowering=False)

    def din(name, shape, dt=F32):
        return nc.dram_tensor(name, list(shape), dt, kind="ExternalInput").ap()

    def dscr(name, shape, dt=BF16):
        return nc.dram_tensor(name, list(shape), dt, kind="Internal").ap()

    xT = din("xT", [128, 8, T])
    wf = {
        "gu1": din("wgu1", [11, 128, 4096]), "d1": din("wd1", [8, 128, 2816]),
        "in": din("win", [NPAN_IN, 128, 2048]), "out": din("wout", [4, 128, 3072]),
        "gu2": din("wgu2", [11, 128, 4096]), "d2": din("wd2", [8, 128, 2816]),
    }
    wb = {k: dscr("b_" + k, v.shape) for k, v in wf.items()}
    small_d = din("small", [128, 1536])
    vec_d = din("vec", [128, NV])
    cst_d = din("cst", [128, NCST], BF16)
    outT = nc.dram_tensor("outT", [128, 8, T], F32, kind="ExternalOutput").ap()

    P = Prog(nc)
    xt = [P.sb("xt0", [128, 8, TT], F32), P.sb("xt1", [128, 8, TT], F32)]
    hb = P.sb("hb", [128, 8, TT], BF16)
    NRING = 4
    wring = [P.sb("wr%d" % i, [128, 4096], BF16) for i in range(NRING)]
    cst = P.sb("cstb", [128, NCST], BF16)
    vec = P.sb("vec", [128, NV], F32)
    vec2 = P.sb("vec2", [128, NV], F32)
    lbv = P.sb("lbv", [128, 8], F32)
    smallf = P.sb("smallf", [128, 1536], F32)
    smallb = P.sb("smallb", [128, 1536], BF16)
    onesb = P.sb("onesb", [128, 128], BF16)
    epsb = P.sb("epsb", [128, 4], F32)
    rstd = P.sb("rstd", [128, TT], F32)
    oA = P.sb("oA", [128, 4, TT], BF16)
    oB = P.sb("oB", [64, 8, TT], BF16)
    S32 = P.sb("S32", [128, 4, 128], F32)
    H32 = P.sb("H32", [64, 8, 64], F32)
    carry = P.sb("carry", [128, 32], F32)
    lora_w = P.sb("lora_w", [32, TT], BF16)
    lora_a = P.sb("lora_a", [32, TT], BF16)
    lora_g = P.sb("lora_g", [96, TT], BF16)
    AR = Arena(P, 94 * 1024)
    psF = [P.ps("psF%d" % i, [128, 1024], F32) for i in range(3)]
    ptB = [P.ps("ptB%d" % i, [128, 1024], BF16) for i in range(2)]
    rr = {"h": 0, "d": 0, "t": 0, "w": 0}

    rr.update({"pool": None, "hA": 0, "hB": 0})

    def nxt():
        if rr["pool"] == "A":
            i = rr["hA"] % 3
            rr["hA"] += 1
        elif rr["pool"] == "B":
            i = 3 + rr["hB"] % 3
            rr["hB"] += 1
        else:
            i = rr["h"] % 6
            rr["h"] += 1
        return psF[i // 2][:, (i % 2) * 512:(i % 2 + 1) * 512], "psF%d_%d" % (i // 2, i % 2)

    def nxt2():
        if rr["pool"] == "A":
            i = 0
        elif rr["pool"] == "B":
            i = 2
        else:
            i = rr["d"] % 3
            rr["d"] += 1
        return psF[i], ["psF%d_0" % i, "psF%d_1" % i]

    def nxtT():
        if rr["pool"] == "A":
            i = 0
        elif rr["pool"] == "B":
            i = 1
        else:
            i = rr["t"] % 2
            rr["t"] += 1
        return ptB[i], "ptB%d" % i

    ident = cst[:, K_ID:K_ID + 128]
    maskI = cst[0:64, K_MI:K_MI + 512]
    maskST = cst[0:64, K_MST:K_MST + 512]
    maskS = cst[0:64, K_MS:K_MS + 512]
    rmask = cst[:, K_RM:K_RM + 512]

    def mm(out, lhsT, rhs, start, stop, r, w):
        P.op("pe", lambda e: e.matmul(out, lhsT=lhsT, rhs=rhs, start=start, stop=stop), r=r, w=w)

    def tr(out, in_, idn, r, w):
        P.op("pe", lambda e: e.transpose(out, in_, idn), r=r, w=w)

    def act(out, in_, func, r, w, bias=None, scale=None):
        kw = {}
        if bias is not None:
            kw["bias"] = bias
        if scale is not None:
            kw["scale"] = scale
        P.op("act", lambda e: e.activation(out=out, in_=in_, func=func, **kw), r=r, w=w)

    def cp(eng, out, in_, r, w):
        if eng == "act":
            P.op("act", lambda e: e.activation(out=out, in_=in_, func=AF.Copy), r=r, w=w)
        else:
            P.op(eng, lambda e: e.tensor_copy(out=out, in_=in_), r=r, w=w)

    def tt(eng, out, in0, in1, op, r, w):
        P.op(eng, lambda e: e.tensor_tensor(out=out, in0=in0, in1=in1, op=op), r=r, w=w)

    def ts(eng, out, in0, s1, s2, op0, op1, r, w):
        if s2 is None and eng == "pool" and op0 == ALU.mult:
            P.op(eng, lambda e: e.tensor_scalar(out=out, in0=in0, scalar1=s1, scalar2=0.0, op0=ALU.mult, op1=ALU.add), r=r, w=w)
        elif s2 is None:
            P.op(eng, lambda e: e.tensor_scalar(out=out, in0=in0, scalar1=s1, scalar2=None, op0=op0), r=r, w=w)
        else:
            P.op(eng, lambda e: e.tensor_scalar(out=out, in0=in0, scalar1=s1, scalar2=s2, op0=op0, op1=op1), r=r, w=w)

    def stt(out, in0, scalar, in1, op0, op1, r, w):
        P.op("dve", lambda e: e.scalar_tensor_tensor(out=out, in0=in0, scalar=scalar, in1=in1, op0=op0, op1=op1), r=r, w=w)

    def scan(out, d0, d1, r, w):
        P.op("dve", lambda e: e.tensor_tensor_scan(out=out, data0=d0, data1=d1, initial=0.0, op0=ALU.mult, op1=ALU.add), r=r, w=w)

    def dma(eng, out, in_, r, w, dsem):
        P.op(eng, lambda e: e.dma_start(out=out, in_=in_), r=r, w=w, dsem=dsem)

    dma("sp", cst[:], cst_d, [], ["cst"], "c0")
    dma("sp", vec[:], vec_d, [], ["vec"], "c1")
    dma("sp", smallf[:], small_d, [], ["smallf"], "c2")
    cast_keys = {}
    for k in ("gu1", "d1", "in", "out", "gu2", "d2"):
        cast_keys[k] = []
        npan = wf[k].shape[0]
        for pn in range(npan):
            key = "wb_%s_%d" % (k, pn)
            allk = [kk_ for v_ in cast_keys.values() for kk_ in v_]
            dma("pool", wb[k][pn], wf[k][pn], allk[-4:-3], [key], "cast%d" % (len(allk) % 4))
            cast_keys[k].append(key)
    cp("act", smallb[:], smallf[:], ["smallf"], ["smallb"])
    P.op("dve", lambda e: e.memset(onesb[:], 1.0), w=["onesb"])
    P.op("dve", lambda e: e.memset(epsb[:, 0:1], NORM_EPS), w=["epsb0"])
    P.op("dve", lambda e: e.memset(epsb[:, 1:2], GN_EPS), w=["epsb1"])
    P.op("dve", lambda e: e.memset(epsb[:, 2:3], 1.0), w=["epsb2"])
    P.op("dve", lambda e: e.memset(S32[:], 0.0), w=["S32_%d" % h for h in range(4)])
    P.op("dve", lambda e: e.memset(H32[:], 0.0), w=["H32_%d" % h for h in range(8)])
    P.op("dve", lambda e: e.memset(carry[:], 0.0), w=["carry"])
    ts("dve", vec2[:], vec[:], -1.0, 1.0, ALU.mult, ALU.add, ["vec"], ["vec2"])
    tt("dve", lbv[:, 0:4], vec[:, V_L0:V_L0 + 4], vec[:, V_L1:V_L1 + 4], ALU.subtract, ["vec"], ["lbv"])
    act(lbv[:, 0:4], lbv[:, 0:4], AF.Sigmoid, ["lbv"], ["lbv"])
    ts("dve", lbv[:, 4:8], lbv[:, 0:4], -1.0, 1.0, ALU.mult, ALU.add, ["lbv"], ["lbv2"])
    CONSTK = ["cst", "vec", "vec2", "lbv", "lbv2", "smallb", "onesb"]

    def load_panel(kind, pn, nelem):
        i = rr["w"] % NRING
        rr["w"] += 1
        key = "wr%d" % i
        dma("sp", wring[i][:, 0:nelem], wb[kind][pn], cast_keys[kind], [key], key)
        return wring[i], key

    def rmsnorm_hb(X, xkeys, gcol, sqb, sqk):
        act(sqb[:], X[:], AF.Square, xkeys, [sqk])
        bank, bk = nxt()
        for kc in range(8):
            mm(bank, onesb[:], sqb[:, kc, :], kc == 0, kc == 7, ["onesb", sqk], [bk])
        act(rstd[:], bank, AF.Ln, [bk, "epsb0"], ["rstd"], bias=epsb[:, 0:1], scale=1.0 / D)
        act(rstd[:], rstd[:], AF.Exp, ["rstd"], ["rstd"], scale=-0.5)
        for kc in range(8):
            stt(hb[:, kc, :], X[:, kc, :], vec[:, gcol + kc:gcol + kc + 1], rstd[:], ALU.mult, ALU.mult,
                [xkeys[kc], "rstd", "vec"], ["hb"])

    def ffn(X, xkeys, gcol, kgu, kd):
        P.barrier()
        AR.reset()
        sqb, sqk = AR.alloc([8, TT], BF16)
        hid, _ = AR.alloc([NHC, TT], BF16)
        sg = [AR.alloc([TT], F32) for _ in range(2)]
        rmsnorm_hb(X, xkeys, gcol, sqb, sqk)
        for pn in range(11):
            wt, wk = load_panel(kgu, pn, 4096)
            wv = wt[:].rearrange("p (g k c) -> p g k c", g=2, k=8)
            for half in range(2):
                m = pn * 2 + half
                bg, bgk = nxt()
                bu, buk = nxt()
                for kc in range(8):
                    mm(bg, wv[:, 0, kc, half * 128:(half + 1) * 128], hb[:, kc, :], kc == 0, kc == 7, [wk, "hb"], [bgk])
                for kc in range(8):
                    mm(bu, wv[:, 1, kc, half * 128:(half + 1) * 128], hb[:, kc, :], kc == 0, kc == 7, [wk, "hb"], [buk])
                sgv, sgk = sg[m % 2]
                act(sgv, bg, AF.Silu, [bgk], [sgk])
                tt("dve", hid[:, m, :], sgv, bu, ALU.mult, [sgk, buk], ["hid%d" % m])
        for mo in range(8):
            wt, wk = load_panel(kd, mo, 2816)
            wv = wt[:, 0:2816].rearrange("p (k c) -> p k c", k=NHC)
            by, byk = nxt()
            for m in range(NHC):
                mm(by, wv[:, m, :], hid[:, m, :], m == 0, m == NHC - 1, [wk, "hid%d" % m], [byk])
            stt(X[:, mo, :], by, 0.5, X[:, mo, :], ALU.mult, ALU.add, [byk, xkeys[mo]], [xkeys[mo]])

    def shift_lerp(bank, bk, R, ccol, mucol, out, outk, raws, tmpl):
        tmp, tk = tmpl
        ck = "carry%d" % ccol
        ts("pool", tmp[0:R, 0:1], carry[0:R, ccol:ccol + 1], vec[0:R, mucol:mucol + 1], None, ALU.mult, None,
           [ck, "carry", "vec"], [tk])
        P.op("act", lambda e: e.activation(out=tmp[0:R, 1:TT], in_=bank[0:R, 0:TT - 1], func=AF.Identity,
                                           scale=vec[0:R, mucol:mucol + 1]), r=[bk, "vec"], w=[tk])
        stt(out, bank[0:R, :], vec2[0:R, mucol:mucol + 1], tmp[0:R, :], ALU.mult, ALU.add, [bk, tk, "vec2"], [outk])
        cp("dve", carry[0:R, ccol:ccol + 1], bank[0:R, TT - 1:TT], [bk], [ck])

    def mixer(X, xkeys):
        P.barrier()
        AR.reset()
        sqb, sqk = AR.alloc([8, TT], BF16)
        rmsnorm_hb(X, xkeys, V_GMX, sqb, sqk)
        f32t = lambda: AR.alloc([TT], F32)
        b16t = lambda: AR.alloc([TT], BF16)
        HS = []
        for _s in range(2):
            d_ = {}
            for nm in ("sqt", "qs", "kf", "logf", "bb", "Dd", "E1", "E2", "t1", "gg"):
                d_[nm] = f32t()
            d_["dv"] = AR.alloc([8], F32)
            for nm in ("Qs", "Ks", "Qb", "KlT", "vf", "osq"):
                d_[nm] = b16t()
            d_["Vtok"] = AR.alloc([8, 128], BF16)
            d_["Kltok"] = AR.alloc([8, 128], BF16)
            d_["scT"] = AR.alloc([8, 64], BF16)
            d_["Sbf"] = AR.alloc([8, 128], BF16)
            HS.append(d_)
        def hgrn_head(h):
            d_ = HS[h % 2]
            (sqt, sqtk), (qs, qsk), (kf, kfk), (logf, logfk), (bb, bbk) = d_["sqt"], d_["qs"], d_["kf"], d_["logf"], d_["bb"]
            (Dd, Ddk), (E1, E1k), (E2, E2k), (t1, t1k), (gg, ggk), (dv, dvk) = d_["Dd"], d_["E1"], d_["E2"], d_["t1"], d_["gg"], d_["dv"]
            (Qs, Qsk), (Ks, Ksk), (Qb, Qbk), (KlT, KlTk), (vf, vfk), (osq, osqk) = d_["Qs"], d_["Ks"], d_["Qb"], d_["KlT"], d_["vf"], d_["osq"]
            (Vtok, Vtokk), (Kltok, Kltokk), (scT, scTk) = d_["Vtok"], d_["Kltok"], d_["scT"]
            Sbf = d_["Sbf"][0]
            sk_ = "S%dbf" % (h % 2)
            wA, wAk = load_panel("in", 2 * h, 2048)
            wAv = wA[:, 0:2048].rearrange("p (k c) -> p k c", k=8)
            bq, bqk = nxt()
            bf_, bfk = nxt()
            for kc in range(8):
                mm(bq, wAv[:, kc, 0:128], hb[:, kc, :], kc == 0, kc == 7, [wAk, "hb"], [bqk])
            for kc in range(8):
                mm(bf_, wAv[:, kc, 128:256], hb[:, kc, :], kc == 0, kc == 7, [wAk, "hb"], [bfk])
            act(sqt, bq, AF.Sigmoid, [bqk], [sqtk])
            tt("dve", qs, bq, sqt, ALU.mult, [bqk, sqtk], [qsk])
            act(kf, bf_, AF.Sigmoid, [bfk], [kfk], scale=-1.0)
            ts("dve", kf, kf, lbv[:, 4 + h:5 + h], None, ALU.mult, None, [kfk, "lbv2"], [kfk])
            wB, wBk = load_panel("in", 2 * h + 1, 2048)
            wBv = wB[:, 0:2048].rearrange("p (k c) -> p k c", k=8)
            bi, bik = nxt()
            bgg, bggk = nxt()
            for kc in range(8):
                mm(bi, wBv[:, kc, 0:128], hb[:, kc, :], kc == 0, kc == 7, [wBk, "hb"], [bik])
            for kc in range(8):
                mm(bgg, wBv[:, kc, 128:256], hb[:, kc, :], kc == 0, kc == 7, [wBk, "hb"], [bggk])
            cp("dve", vf, bi, [bik], [vfk])
            act(sqt, bgg, AF.Sigmoid, [bggk], [sqtk])
            tt("dve", gg, bgg, sqt, ALU.mult, [bggk, sqtk], [ggk])
            act(logf, kf, AF.Ln, [kfk, "epsb2"], [logfk], bias=epsb[:, 2:3], scale=-1.0)
            scan(bb, rmask, logf, [logfk, "cst"], [bbk])
            b3 = bb.rearrange("p (c t) -> p c t", t=64)
            D3 = Dd.rearrange("p (c t) -> p c t", t=64)
            tt("dve", D3, b3, b3[:, :, 32:33].broadcast_to([128, 8, 64]), ALU.subtract, [bbk], [Ddk])
            act(E1, Dd, AF.Exp, [Ddk], [E1k])
            act(E2, Dd, AF.Exp, [Ddk], [E2k], scale=-1.0)
            tt("pool", Qs, qs, E1, ALU.mult, [qsk, E1k], [Qsk])
            tt("dve", Ks, kf, E2, ALU.mult, [kfk, E2k], [Ksk])
            act(E1, bb, AF.Exp, [bbk], [E1k])
            tt("pool", Qb, qs, E1, ALU.mult, [qsk, E1k], [Qbk])
            tt("dve", D3, b3, b3[:, :, 63:64].broadcast_to([128, 8, 64]), ALU.subtract, [bbk], [Ddk])
            act(E2, Dd, AF.Exp, [Ddk], [E2k], scale=-1.0)
            tt("dve", KlT, kf, E2, ALU.mult, [kfk, E2k], [KlTk])
            act(dv, b3[:, :, 63], AF.Exp, [bbk], [dvk])
            tb, tbk = nxtT()
            for c in range(8):
                tr(tb[0:64, c * 128:(c + 1) * 128], vf[:, c * 64:(c + 1) * 64], ident, [vfk, "cst"], [tbk])
            cp("act", Vtok[0:64], tb[0:64, :].rearrange("p (c v) -> p c v", v=128), [tbk], [Vtokk])
            tb2, tb2k = nxtT()
            for c in range(8):
                tr(tb2[0:64, c * 128:(c + 1) * 128], KlT[:, c * 64:(c + 1) * 64], ident, [KlTk, "cst"], [tb2k])
            cp("dve", Kltok[0:64], tb2[0:64, :].rearrange("p (c v) -> p c v", v=128), [tb2k], [Kltokk])
            bs, bsk = nxt()
            for c in range(8):
                mm(bs[0:64, c * 64:(c + 1) * 64], Ks[:, c * 64:(c + 1) * 64], Qs[:, c * 64:(c + 1) * 64], True, True, [Ksk, Qsk], [bsk])
            tt("dve", scT[0:64], bs[0:64, :].rearrange("p (c t) -> p c t", t=64), maskI.rearrange("p (c t) -> p c t", t=64),
               ALU.mult, [bsk, "cst"], [scTk])
            bu2, bu2k = nxt2()
            for c in range(8):
                mm(bu2[:, c * 128:(c + 1) * 128], Kltok[0:64, c, :], Vtok[0:64, c, :], True, True, [Kltokk, Vtokk], [bu2k[c // 4]])
            cp("act", Sbf[:, 0, :], S32[:, h, :], ["S32_%d" % h], [sk_ + "0"])
            for c in range(8):
                stt(S32[:, h, :], S32[:, h, :], dv[:, c:c + 1], bu2[:, c * 128:(c + 1) * 128], ALU.mult, ALU.add,
                    ["S32_%d" % h, dvk, bu2k[c // 4]], ["S32_%d" % h])
                if c < 7:
                    cp("act" if c % 2 else "pool", Sbf[:, c + 1, :], S32[:, h, :], ["S32_%d" % h], [sk_ + str(c + 1)])
            bo, bok = nxt()
            for c in range(8):
                mm(bo[:, c * 64:(c + 1) * 64], Vtok[0:64, c, :], scT[0:64, c, :], True, False, [Vtokk, scTk], [bok])
                mm(bo[:, c * 64:(c + 1) * 64], Sbf[:, c, :], Qb[:, c * 64:(c + 1) * 64], False, True, [sk_ + str(c), Qbk], [bok])
            act(osq, bo, AF.Square, [bok], [osqk])
            bss, bssk = nxt()
            mm(bss, onesb[:], osq, True, True, ["onesb", osqk], [bssk])
            act(t1, bss, AF.Ln, [bssk, "epsb0"], [t1k], bias=epsb[:, 0:1], scale=1.0 / 128)
            act(t1, t1, AF.Exp, [t1k], [t1k], scale=-0.5)
            stt(t1, bo, vec[:, V_ON + h:V_ON + h + 1], t1, ALU.mult, ALU.mult, [bok, t1k, "vec"], [t1k])
            tt("dve", oA[:, h, :], t1, gg, ALU.mult, [t1k, ggk], ["oA%d" % h])

        for h0 in (0, 2):
            rr["pool"] = "A"
            la = P.record(hgrn_head, h0)
            rr["pool"] = "B"
            lb = P.record(hgrn_head, h0 + 1)
            rr["pool"] = None
            P.replay(la, lb)

        P.barrier()
        AR.reset()
        if STAGE == "hgrn" or CUT < 99:
            P.op("dve", lambda e: e.memset(oB[:], 0.0), w=["oB%d" % h for h in range(8)])
        raws = [AR.alloc([TT + 1], F32) for _ in range(2)]
        tmpl = AR.alloc([TT], F32)
        lt, ltk = AR.alloc([TT], F32)
        f32t = lambda: AR.alloc([TT], F32)
        b16t = lambda: AR.alloc([TT], BF16)
        r32, r32k = f32t()
        k32, k32k = f32t()
        v32, v32k = f32t()
        sgw, sgwk = f32t()
        asg, asgk = f32t()
        g32S = [f32t() for _ in range(2)]
        kkn, kknk = f32t()
        kmod, kmodk = f32t()
        cs, csk = f32t()
        Dd, Ddk = f32t()
        E1, E1k = f32t()
        E2, E2k = f32t()
        al, alk = f32t()
        bet, betk = f32t()
        bsumS = [f32t() for _ in range(2)]
        E1b, E1bk = f32t()
        y32, y32k = f32t()
        gamS = [AR.alloc([8], F32) for _ in range(2)]
        sq16, sq16k = b16t()
        sq16b, sq16bk = b16t()
        abar, abark = b16t()
        rbarS = [b16t() for _ in range(2)]
        Bh, Bhk = b16t()
        Kh, Khk = b16t()
        vb, vbk = b16t()
        ARflat, ARfk = AR.alloc([1024], BF16)
        BKflat, BKfk = AR.alloc([1024], BF16)
        ARf = ARflat.rearrange("p (c j t) -> p c j t", j=2, t=64)
        BKf = BKflat.rearrange("p (c j t) -> p c j t", j=2, t=64)
        tokbS = [[AR.alloc([8, 64], BF16) for _ in range(4)] for _ in range(2)]
        P0T, P0Tk = AR.alloc([8, 64], BF16)
        ArbTS = [AR.alloc([8, 64], BF16) for _ in range(2)]
        ArkTS = [AR.alloc([8, 64], BF16) for _ in range(2)]
        AakS = [AR.alloc([8, 64], BF16) for _ in range(2)]
        PmS0 = [AR.alloc([8, 64], BF16) for _ in range(2)]
        PTmS0 = [AR.alloc([8, 64], BF16) for _ in range(2)]
        TTmS0 = [AR.alloc([8, 64], BF16) for _ in range(2)]
        Pm1 = AR.alloc([8, 64], BF16)
        PTm1 = AR.alloc([8, 64], BF16)
        TTm1 = AR.alloc([8, 64], BF16)
        W1T, W1Tk = AR.alloc([8, 64], BF16)
        TAk, TAkk = AR.alloc([8, 64], BF16)
        Usb, _ = AR.alloc([8, 64], BF16)
        Hbf, _ = AR.alloc([8, 64], BF16)
        id64 = ident[0:64, 0:64]
        ones64 = onesb[0:64, 0:64]
        v3 = lambda a: a.rearrange("p (c t) -> p c t", t=64)

        wL, wLk = load_panel("in", 16, 2048)
        wLv = wL[:, 0:2048].rearrange("p (k c) -> p k c", k=8)
        for (c0_, c1_, R, ccol, mucol, dst, dstk, fn) in (
            (0, 32, 32, 24, V_MUWL, lora_w, "lora_w", AF.Tanh),
            (32, 64, 32, 25, V_MUAL, lora_a, "lora_a", AF.Copy),
            (64, 160, 96, 26, V_MUGL, lora_g, "lora_g", AF.Sigmoid),
        ):
            bk_, bkk = nxt()
            for kc in range(8):
                mm(bk_[0:R, :], wLv[:, kc, c0_:c1_], hb[:, kc, :], kc == 0, kc == 7, [wLk, "hb"], [bkk])
            shift_lerp(bk_, bkk, R, ccol, mucol, lt[0:R, :], ltk, raws[0], tmpl)
            act(dst[0:R, :], lt[0:R, :], fn, [ltk], [dstk])

        def stageA(h):
            sl = h % 2
            (rbar, rbark), (gam, gamk), (bsum, bsumk), (g32, g32k) = rbarS[sl], gamS[sl], bsumS[sl], g32S[sl]
            (ArbT, ArbTk), (ArkT, ArkTk), (Aak, Aakk) = ArbTS[sl], ArkTS[sl], AakS[sl]
            tokb = tokbS[sl]
            Pm = [PmS0[sl], Pm1]
            PTm = [PTmS0[sl], PTm1]
            TTm = [TTmS0[sl], TTm1]
            gi, e = h // 2, h % 2
            if e == 0:
                w1_, w1k = load_panel("in", 8 + 2 * gi, 2048)
                w2_, w2k = load_panel("in", 9 + 2 * gi, 2048)
                pw["v"] = (w1_[:, 0:2048].rearrange("p (k c) -> p k c", k=8), w1k,
                           w2_[:, 0:2048].rearrange("p (k c) -> p k c", k=8), w2k)
            w1v, w1k, w2v, w2k = pw["v"]
            for qi, (wv_, wk_, coff, mu0, dst, dstk) in enumerate((
                (w1v, w1k, e * 64, V_MUR, r32, r32k),
                (w1v, w1k, 128 + e * 64, V_MUK, k32, k32k),
                (w2v, w2k, e * 64, V_MUV, v32, v32k),
            )):
                bk_, bkk = nxt()
                for kc in range(8):
                    mm(bk_[0:64, :], wv_[:, kc, coff:coff + 64], hb[:, kc, :], kc == 0, kc == 7, [wk_, "hb"], [bkk])
                shift_lerp(bk_, bkk, 64, qi * 8 + h, mu0 + h, dst[0:64, :], dstk, raws[(qi + 1) % 2], tmpl)
            bz, bzk = nxt()
            mm(bz[0:64, :], smallb[0:32, h * 64:(h + 1) * 64], lora_w[:], True, True, ["smallb", "lora_w"], [bzk])
            act(sgw[0:64, :], bz[0:64, :], AF.Sigmoid, [bzk, "vec"], [sgwk], bias=vec[0:64, V_W0 + h:V_W0 + h + 1])
            bz, bzk = nxt()
            mm(bz[0:64, :], smallb[0:32, 512 + h * 64:512 + (h + 1) * 64], lora_a[:], True, True, ["smallb", "lora_a"], [bzk])
            act(asg[0:64, :], bz[0:64, :], AF.Sigmoid, [bzk, "vec"], [asgk], bias=vec[0:64, V_A0 + h:V_A0 + h + 1])
            bz, bzk = nxt()
            mm(bz[0:64, :], smallb[0:96, 1024 + h * 64:1024 + (h + 1) * 64], lora_g[:], True, True, ["smallb", "lora_g"], [bzk])
            cp("act", g32[0:64, :], bz[0:64, :], [bzk], [g32k])
            act(sq16[0:64, :], k32[0:64, :], AF.Square, [k32k, "vec"], [sq16k], scale=vec[0:64, V_KK + h:V_KK + h + 1])
            ts("pool", kkn[0:64, :], k32[0:64, :], vec[0:64, V_KK + h:V_KK + h + 1], None, ALU.mult, None, [k32k, "vec"], [kknk])
            bz, bzk = nxt()
            mm(bz[0:64, :], ones64, sq16[0:64, :], True, True, ["onesb", sq16k], [bzk])
            ts("dve", E1[0:64, :], bz[0:64, :], 1e-19, None, ALU.max, None, [bzk], [E1k])
            act(E1[0:64, :], E1[0:64, :], AF.Ln, [E1k], [E1k])
            act(E1[0:64, :], E1[0:64, :], AF.Exp, [E1k], [E1k], scale=-0.5)
            tt("dve", kkn[0:64, :], kkn[0:64, :], E1[0:64, :], ALU.mult, [kknk, E1k], [kknk])
            ts("dve", kmod[0:64, :], asg[0:64, :], -1.0, vec[0:64, V_KA + h:V_KA + h + 1], ALU.add, ALU.mult, [asgk, "vec"], [kmodk])
            stt(kmod[0:64, :], kmod[0:64, :], 1.0, k32[0:64, :], ALU.add, ALU.mult, [kmodk, k32k], [kmodk])
            stt(sq16[0:64, :], r32[0:64, :], vec[0:64, V_RK + h:V_RK + h + 1], kmod[0:64, :], ALU.mult, ALU.mult,
                [r32k, kmodk, "vec"], [sq16k])
            bz, bzk = nxt()
            mm(bz[0:64, :], ones64, sq16[0:64, :], True, True, ["onesb", sq16k], [bzk])
            tt("dve", bsum[0:64, :], bz[0:64, :], v32[0:64, :], ALU.mult, [bzk, v32k], [bsumk])
            tt("pool", bet[0:64, :], kkn[0:64, :], asg[0:64, :], ALU.mult, [kknk, asgk], [betk])
            scan(cs[0:64, :], rmask[0:64, :], sgw[0:64, :], [sgwk, "cst"], [csk])
            cs3 = v3(cs[0:64, :])
            D3 = v3(Dd[0:64, :])
            act(E1[0:64, :], sgw[0:64, :], AF.Exp, [sgwk], [E1k], scale=C0)
            tt("dve", al[0:64, :], kkn[0:64, :], E1[0:64, :], ALU.mult, [kknk, E1k], [alk])
            tt("dve", D3, cs3, cs3[:, :, 32:33].broadcast_to([64, 8, 64]), ALU.subtract, [csk], [Ddk])
            act(E1[0:64, :], Dd[0:64, :], AF.Exp, [Ddk], [E1k], scale=-C0)
            act(E2[0:64, :], Dd[0:64, :], AF.Exp, [Ddk], [E2k], scale=C0)
            stt(ARf[0:64, :, 0, :], v3(al[0:64, :]), -1.0, v3(E1[0:64, :]), ALU.mult, ALU.mult, [alk, E1k], [ARfk])
            tt("pool", ARf[0:64, :, 1, :], v3(r32[0:64, :]), v3(E1[0:64, :]), ALU.mult, [r32k, E1k], [ARfk])
            tt("dve", BKf[0:64, :, 0, :], v3(bet[0:64, :]), v3(E2[0:64, :]), ALU.mult, [betk, E2k], [BKfk])
            tt("pool", BKf[0:64, :, 1, :], v3(kmod[0:64, :]), v3(E2[0:64, :]), ALU.mult, [kmodk, E2k], [BKfk])
            act(E1[0:64, :], cs[0:64, :], AF.Exp, [csk], [E1k], scale=-C0)
            stt(abar[0:64, :], al[0:64, :], -1.0, E1[0:64, :], ALU.mult, ALU.mult, [alk, E1k], [abark])
            tt("pool", rbar[0:64, :], r32[0:64, :], E1[0:64, :], ALU.mult, [r32k, E1k], [rbark])
            tt("dve", D3, cs3, cs3[:, :, 63:64].broadcast_to([64, 8, 64]), ALU.subtract, [csk], [Ddk])
            act(E2[0:64, :], Dd[0:64, :], AF.Exp, [Ddk], [E2k], scale=C0)
            tt("dve", Bh[0:64, :], bet[0:64, :], E2[0:64, :], ALU.mult, [betk, E2k], [Bhk])
            tt("pool", Kh[0:64, :], kmod[0:64, :], E2[0:64, :], ALU.mult, [kmodk, E2k], [Khk])
            act(gam[0:64, :], cs3[:, :, 63], AF.Exp, [csk], [gamk], scale=-C0)
            cp("pool", vb[0:64, :], v32[0:64, :], [v32k], [vbk])
            for qi, (src, srck) in enumerate(((abar, abark), (Bh, Bhk), (Kh, Khk), (vb, vbk))):
                tb, tbk = nxtT()
                for c in range(8):
                    tr(tb[0:64, c * 64:(c + 1) * 64], src[0:64, c * 64:(c + 1) * 64], id64, [srck, "cst"], [tbk])
                cp("act" if qi % 2 else "dve", tokb[qi][0][0:64], tb[0:64, 0:512].rearrange("p (c k) -> p c k", k=64), [tbk], [tokb[qi][1]])
            (Atok, Atokk), (Btok, Btokk), (Ktok, Ktokk), (Vtk, Vtkk) = tokb
            m1, m1k = nxt2()
            for c in range(8):
                mm(m1[0:64, c * 128:(c + 1) * 128], BKf[0:64, c, 0, :], ARflat[0:64, c * 128:(c + 1) * 128], True, True, [BKfk, ARfk], [m1k[c // 4]])
            m13 = m1[0:64, :].rearrange("p (c j t) -> p c j t", j=2, t=64)
            mI3 = maskI.rearrange("p (c t) -> p c t", t=64)
            mST3 = maskST.rearrange("p (c t) -> p c t", t=64)
            mS3 = maskS.rearrange("p (c t) -> p c t", t=64)
            tt("dve", P0T[0:64], m13[:, :, 0, :], mST3, ALU.mult, m1k + ["cst"], [P0Tk])
            tt("dve", ArbT[0:64], m13[:, :, 1, :], mI3, ALU.mult, m1k + ["cst"], [ArbTk])
            m2, m2k = nxt()
            for c in range(8):
                mm(m2[0:64, c * 64:(c + 1) * 64], BKf[0:64, c, 1, :], ARf[0:64, c, 1, :], True, True, [BKfk, ARfk], [m2k])
            tt("dve", ArkT[0:64], m2[0:64, :].rearrange("p (c t) -> p c t", t=64), mI3, ALU.mult, [m2k, "cst"], [ArkTk])
            m3, m3k = nxt2()
            for c in range(8):
                mm(m3[0:64, c * 128:(c + 1) * 128], ARf[0:64, c, 0, :], BKflat[0:64, c * 128:(c + 1) * 128], True, True, [BKfk, ARfk], [m3k[c // 4]])
            m33 = m3[0:64, :].rearrange("p (c j t) -> p c j t", j=2, t=64)
            tt("dve", Pm[0][0][0:64], m33[:, :, 0, :], mS3, ALU.mult, m3k + ["cst"], [Pm[0][1]])
            tt("dve", Aak[0:64], m33[:, :, 1, :], mS3, ALU.mult, m3k + ["cst"], [Aakk])
            cp("pool", PTm[0][0][0:64], P0T[0:64], [P0Tk], [PTm[0][1]])
            tt("pool", TTm[0][0][0:64], P0T[0:64], id64.rearrange("p (o t) -> p o t", o=1).broadcast_to([64, 8, 64]), ALU.add,
               [P0Tk, "cst"], [TTm[0][1]])
        def stageB(h):
            sl = h % 2
            (rbar, rbark), (gam, gamk), (bsum, bsumk), (g32, g32k) = rbarS[sl], gamS[sl], bsumS[sl], g32S[sl]
            (ArbT, ArbTk), (ArkT, ArkTk), (Aak, Aakk) = ArbTS[sl], ArkTS[sl], AakS[sl]
            tokb = tokbS[sl]
            Pm = [PmS0[sl], Pm1]
            PTm = [PTmS0[sl], PTm1]
            TTm = [TTmS0[sl], TTm1]
            (Atok, Atokk), (Btok, Btokk), (Ktok, Ktokk), (Vtk, Vtkk) = tokb
            (E1, E1k), (sq16, sq16k) = (E1b, E1bk), (sq16b, sq16bk)
            for j in range(1, 6):
                (Pc, Pck), (Pn, Pnk) = Pm[(j - 1) % 2], Pm[j % 2]
                (PTc, PTck), (PTn, PTnk) = PTm[(j - 1) % 2], PTm[j % 2]
                (Tc, Tck), (Tn, Tnk) = TTm[(j - 1) % 2], TTm[j % 2]
                bp, bpk = nxt()
                for c in range(8):
                    mm(bp[0:64, c * 64:(c + 1) * 64], PTc[0:64, c, :], Pc[0:64, c, :], True, True, [PTck, Pck], [bpk])
                cp("act", Pn[0:64], bp[0:64, :].rearrange("p (c t) -> p c t", t=64), [bpk], [Pnk])
                if j < 5:
                    bq_, bqk_ = nxt()
                    for c in range(8):
                        mm(bq_[0:64, c * 64:(c + 1) * 64], Pc[0:64, c, :], PTc[0:64, c, :], True, True, [PTck, Pck], [bqk_])
                    cp("dve", PTn[0:64], bq_[0:64, :].rearrange("p (c t) -> p c t", t=64), [bqk_], [PTnk])
                bt_, btk_ = nxt()
                for c in range(8):
                    mm(bt_[0:64, c * 64:(c + 1) * 64], Pn[0:64, c, :], Tc[0:64, c, :], True, True, [Pnk, Tck], [btk_])
                tt("dve", Tn[0:64], bt_[0:64, :].rearrange("p (c t) -> p c t", t=64), Tc[0:64], ALU.add, [btk_, Tck], [Tnk])
            Tf, Tfk = TTm[5 % 2]
            bw_, bwk_ = nxt()
            for c in range(8):
                mm(bw_[0:64, c * 64:(c + 1) * 64], Atok[0:64, c, :], Tf[0:64, c, :], True, True, [Atokk, Tfk], [bwk_])
            cp("act", W1T[0:64], bw_[0:64, :].rearrange("p (c t) -> p c t", t=64), [bwk_], [W1Tk])
            bw_, bwk_ = nxt()
            for c in range(8):
                mm(bw_[0:64, c * 64:(c + 1) * 64], Aak[0:64, c, :], Tf[0:64, c, :], True, True, [Aakk, Tfk], [bwk_])
            cp("dve", TAk[0:64], bw_[0:64, :].rearrange("p (c t) -> p c t", t=64), [bwk_], [TAkk])
            hk = "H32_%d" % h
            cp("act", Hbf[0:64, 0, :], H32[:, h, :], [hk], ["Hbf0"])
            for c in range(8):
                bU, bUk = nxt()
                mm(bU[0:64, 0:64], TAk[0:64, c, :], Vtk[0:64, c, :], True, False, [TAkk, Vtkk], [bUk])
                mm(bU[0:64, 0:64], W1T[0:64, c, :], Hbf[0:64, c, :], False, True, [W1Tk, "Hbf%d" % c], [bUk])
                cp("act", Usb[0:64, c, :], bU[0:64, 0:64], [bUk], ["Usb%d" % c])
                bH, bHk = nxt()
                mm(bH[0:64, 0:64], Btok[0:64, c, :], Usb[0:64, c, :], True, False, [Btokk, "Usb%d" % c], [bHk])
                mm(bH[0:64, 0:64], Ktok[0:64, c, :], Vtk[0:64, c, :], False, True, [Ktokk, Vtkk], [bHk])
                if c < 7:
                    stt(Hbf[0:64, c + 1, :], H32[:, h, :], gam[0:64, c:c + 1], bH[0:64, 0:64], ALU.mult, ALU.add,
                        [hk, gamk, bHk], ["Hbf%d" % (c + 1)])
                stt(H32[:, h, :], H32[:, h, :], gam[0:64, c:c + 1], bH[0:64, 0:64], ALU.mult, ALU.add, [hk, gamk, bHk], [hk])
            bY, bYk = nxt()
            for c in range(8):
                o_ = bY[0:64, c * 64:(c + 1) * 64]
                mm(o_, Hbf[0:64, c, :], rbar[0:64, c * 64:(c + 1) * 64], True, False, ["Hbf%d" % c, rbark], [bYk])
                mm(o_, Usb[0:64, c, :], ArbT[0:64, c, :], False, False, ["Usb%d" % c, ArbTk], [bYk])
                mm(o_, Vtk[0:64, c, :], ArkT[0:64, c, :], False, True, [Vtkk, ArkTk], [bYk])
            GN = int(os.environ.get("MK_GN", "99"))
            cp("act", y32[0:64, :], bY[0:64, :], [bYk], [y32k])
            cp("dve", sq16[0:64, :], y32[0:64, :], [y32k], [sq16k])
            bz, bzk = nxt()
            mm(bz[0:64, :], ones64, sq16[0:64, :], True, True, ["onesb", sq16k], [bzk])
            if GN >= 1:
                stt(y32[0:64, :], bz[0:64, :], -1.0 / 64, y32[0:64, :], ALU.mult, ALU.add, [bzk, y32k], [y32k])
                act(sq16[0:64, :], y32[0:64, :], AF.Square, [y32k], [sq16k])
                bz, bzk = nxt()
                mm(bz[0:64, :], ones64, sq16[0:64, :], True, True, ["onesb", sq16k], [bzk])
            if GN >= 2:
                act(E1[0:64, :], bz[0:64, :], AF.Ln, [bzk, "epsb1"], [E1k], bias=epsb[0:64, 1:2], scale=1.0 / 64)
                act(E1[0:64, :], E1[0:64, :], AF.Exp, [E1k], [E1k], scale=-0.5)
                tt("dve", y32[0:64, :], y32[0:64, :], E1[0:64, :], ALU.mult, [y32k, E1k], [y32k])
            if GN >= 3:
                ts("dve", y32[0:64, :], y32[0:64, :], vec[0:64, V_GNW + h:V_GNW + h + 1], vec[0:64, V_GNB + h:V_GNB + h + 1],
                   ALU.mult, ALU.add, [y32k, "vec"], [y32k])
            if GN >= 4:
                tt("pool", y32[0:64, :], y32[0:64, :], bsum[0:64, :], ALU.add, [y32k, bsumk], [y32k])
            if GN >= 5:
                tt("dve", oB[:, h, :], y32[0:64, :], g32[0:64, :], ALU.mult, [y32k, g32k], ["oB%d" % h])

        pw = {}
        NH = 8 if STAGE != "hgrn" else 0

        def recA(h):
            rr["pool"] = "A"
            l = P.record(stageA, h)
            rr["pool"] = None
            return l

        def recB(h):
            rr["pool"] = "B"
            l = P.record(stageB, h)
            rr["pool"] = None
            return l

        if NH:
            P.replay(recA(0))
        for h in range(NH):
            la = recA(h + 1) if h + 1 < NH else []
            lb = recB(h)
            P.replay(lb, la)

        for pn in range(4):
            wt, wk = load_panel("out", pn, 3072)
            wv = wt[:, 0:3072].rearrange("p (k c) -> p k c", k=12)
            for half in range(2):
                mo = pn * 2 + half
                bo, bok = nxt()
                for kc in range(4):
                    mm(bo, wv[:, kc, half * 128:(half + 1) * 128], oA[:, kc, :], kc == 0, False, [wk, "oA%d" % kc], [bok])
                for h in range(8):
                    mm(bo, wv[0:64, 4 + h, half * 128:(half + 1) * 128], oB[:, h, :], False, h == 7, [wk, "oB%d" % h], [bok])
                tt("dve", X[:, mo, :], bo, X[:, mo, :], ALU.add, [bok, xkeys[mo]], [xkeys[mo]])

    dma("sp", xt[0][:], xT[:, :, 0:TT], [], ["x0_%d" % k for k in range(8)], "x0")
    for it in range(NT):
        cur = it % 2
        X = xt[cur]
        xkeys = ["x%d_%d" % (cur, k) for k in range(8)]
        if it + 1 < NT:
            nx = 1 - cur
            dma("sp", xt[nx][:], xT[:, :, (it + 1) * TT:(it + 2) * TT], [], ["x%d_%d" % (nx, k) for k in range(8)], "x%d" % nx)
        ffn(X, xkeys, V_GF1, "gu1", "d1")
        if STAGE != "ffn1":
            mixer(X, xkeys)
            ffn(X, xkeys, V_GF2, "gu2", "d2")
        P.barrier()
        AR.reset()
        sqb, sqk = AR.alloc([8, TT], BF16)
        act(sqb[:], X[:], AF.Square, xkeys, [sqk])
        bank, bk = nxt()
        for kc in range(8):
            mm(bank, onesb[:], sqb[:, kc, :], kc == 0, kc == 7, ["onesb", sqk], [bk])
        act(rstd[:], bank, AF.Ln, [bk, "epsb0"], ["rstd"], bias=epsb[:, 0:1], scale=1.0 / D)
        act(rstd[:], rstd[:], AF.Exp, ["rstd"], ["rstd"], scale=-0.5)
        for kc in range(8):
            stt(X[:, kc, :], X[:, kc, :], vec[:, V_GFN + kc:V_GFN + kc + 1], rstd[:], ALU.mult, ALU.mult,
                [xkeys[kc], "rstd", "vec"], [xkeys[kc]])
        dma("pool", outT[:, :, it * TT:(it + 1) * TT], X[:], xkeys, [], "st")

    P.emit(final_waits=[("st", "pool")])
    P.es.close()
    return nc


def _host_layout(inp):
    f = np.float32
    def panels_gu(wg, wu):
        a = np.stack([wg, wu], 0).reshape(2, 8, 128, 11, 256)
        return np.ascontiguousarray(a.transpose(3, 2, 0, 1, 4)).reshape(11, 128, 4096)
    def panels_d(wd):
        a = wd.reshape(NHC, 128, 8, 128)
        return np.ascontiguousarray(a.transpose(2, 1, 0, 3)).reshape(8, 128, 2816)
    w_in = inp["w_in"][0]
    cols = []
    for h in range(4):
        cols += list(range(h * 128, (h + 1) * 128)) + list(range(512 + h * 128, 512 + (h + 1) * 128))
        cols += list(range(1024 + h * 128, 1024 + (h + 1) * 128)) + list(range(1536 + h * 128, 1536 + (h + 1) * 128))
    for gi in range(4):
        cols += list(range(2048 + gi * 128, 2048 + (gi + 1) * 128)) + list(range(2560 + gi * 128, 2560 + (gi + 1) * 128))
        cols += list(range(3072 + gi * 128, 3072 + (gi + 1) * 128)) + [-1] * 128
    cols += list(range(3584, 3744)) + [-1] * 96
    cols = np.array(cols)
    assert cols.size == NPAN_IN * 256
    wpad = np.concatenate([w_in, np.zeros((D, 1), f)], axis=1)
    wperm = wpad[:, cols]
    a = wperm.reshape(8, 128, NPAN_IN, 256)
    win = np.ascontiguousarray(a.transpose(2, 1, 0, 3)).reshape(NPAN_IN, 128, 2048)
    w_out = inp["w_out"][0]
    wo = np.zeros((12, 128, D), f)
    wo[0:4] = w_out[0:512].reshape(4, 128, D)
    wo[4:12, 0:64] = w_out[512:1024].reshape(8, 64, D)
    a = wo.reshape(12, 128, 4, 256)
    wout = np.ascontiguousarray(a.transpose(2, 1, 0, 3)).reshape(4, 128, 3072)
    small = np.zeros((128, 1536), f)
    small[0:32, 0:512] = inp["rwkv_w2"][0]
    small[0:32, 512:1024] = inp["rwkv_a2"][0]
    small[0:96, 1024:1536] = inp["rwkv_g2"][0]
    vec = np.zeros((128, NV), f)
    def put_d(c0, v):
        vec[:, c0:c0 + 8] = v.reshape(8, 128).T
    put_d(V_GF1, inp["ffn1_norm"][0]); put_d(V_GMX, inp["mix_norm"][0])
    put_d(V_GF2, inp["ffn2_norm"][0]); put_d(V_GFN, inp["final_norm"])
    vec[:, V_L0:V_L0 + 4] = inp["hgrn_lb_logits"][0].reshape(4, 128).T
    vec[:, V_L1:V_L1 + 4] = inp["hgrn_lb_logits"][1].reshape(4, 128).T
    vec[:, V_ON:V_ON + 4] = inp["hgrn_out_norm"][0].reshape(4, 128).T
    mu = inp["rwkv_shift_mu"][0]
    def put_h(c0, v):
        vec[0:64, c0:c0 + 8] = v.reshape(8, 64).T
    put_h(V_MUR, mu[0:512]); put_h(V_MUK, mu[512:1024]); put_h(V_MUV, mu[1024:1536])
    put_h(V_W0, inp["rwkv_w0"][0]); put_h(V_A0, inp["rwkv_a0"][0]); put_h(V_KK, inp["rwkv_k_k"][0])
    put_h(V_KA, inp["rwkv_k_a"][0]); put_h(V_RK, inp["rwkv_r_k"][0].reshape(-1))
    put_h(V_GNW, inp["rwkv_gn_w"][0]); put_h(V_GNB, inp["rwkv_gn_b"][0])
    vec[0:32, V_MUWL] = mu[1536:1568]; vec[0:32, V_MUAL] = mu[1568:1600]; vec[0:96, V_MUGL] = mu[1600:1696]
    cst = np.zeros((128, NCST), f)
    cst[:, K_ID:K_ID + 128] = np.eye(128, dtype=f)
    s = np.arange(64)[:, None]; t = np.arange(64)[None, :]
    cst[0:64, K_MI:K_MI + 512] = np.tile((s <= t).astype(f), (1, 8))
    cst[0:64, K_MST:K_MST + 512] = np.tile((s < t).astype(f), (1, 8))
    cst[0:64, K_MS:K_MS + 512] = np.tile((s > t).astype(f), (1, 8))
    rm = np.ones((128, 512), f); rm[:, ::64] = 0
    cst[:, K_RM:K_RM + 512] = rm
    shared = {
        "wgu1": panels_gu(inp["ffn1_w_gate"][0], inp["ffn1_w_up"][0]), "wd1": panels_d(inp["ffn1_w_down"][0]),
        "win": win, "wout": wout,
        "wgu2": panels_gu(inp["ffn2_w_gate"][0], inp["ffn2_w_up"][0]), "wd2": panels_d(inp["ffn2_w_down"][0]),
        "small": small, "vec": vec, "cst": cst.astype(ml_dtypes.bfloat16),
    }
    return shared


def _x_layout(xb):
    T = xb.shape[0]
    return np.ascontiguousarray(xb.reshape(T, 8, 128).transpose(2, 1, 0))


_NC_CACHE = {}


def kernel(**inputs):
    inp = {k: np.asarray(v) for k, v in inputs.items()}
    x = inp["x"]
    B, T, _ = x.shape
    shared = _host_layout(inp)
    if T not in _NC_CACHE:
        _NC_CACHE[T] = build(T)
    nc = _NC_CACHE[T]
    in_maps = []
    for b in range(B):
        m = dict(shared)
        m["xT"] = _x_layout(x[b])
        in_maps.append(m)
    res = run_bass_kernel_spmd(nc, in_maps, core_ids=list(range(B)))
    out = np.empty((B, T, D), np.float32)
    for b in range(B):
        o = np.asarray(res.results[b]["outT"])
        out[b] = o.transpose(2, 1, 0).reshape(T, D)
    return out
```
